# Optimizing a Trainium2 kernel written in Bass

```python
import jax, jax.numpy as jnp
from jax import lax
import numpy as np

D_MODEL = 1024
BATCH = 4
SEQ = 8192
DEPTH = 4

CHUNK = 64
D_MIX = D_MODEL
N_GROUPS = 4
GW = D_MIX // N_GROUPS
HEAD_DIM = 64
N_HEADS_GROUP = GW // HEAD_DIM
ATTN_LEFT_CHUNKS = 8
N_BAND = ATTN_LEFT_CHUNKS + 1
REL_CLIP = 256
ATTN_SCALE = HEAD_DIM ** -0.5
NEG_INF = -1e30
RWKV_W_RANK = 32
RWKV_A_RANK = 32
RWKV_G_RANK = 64
RWKV_GN_EPS = HEAD_DIM * 1e-5
LRU_CONV = 4
LRU_C = 8.0
LRU_BLOCKS = N_HEADS_GROUP
LRU_BLOCK_DIM = GW // LRU_BLOCKS
ATTN_COLS = 3 * GW
HGRN_COLS = 4 * GW
RWKV_COLS = 3 * GW + RWKV_W_RANK + RWKV_A_RANK + RWKV_G_RANK
LRU_COLS = 2 * GW
D_IN = ATTN_COLS + HGRN_COLS + RWKV_COLS + LRU_COLS
MLP_HIDDEN = 4 * D_MODEL
RMS_EPS = 1e-6

kernel_name = 'hybrid_chunk_causal_block'


def split_cols(t, sizes):
    offs = np.cumsum(sizes)[:-1].tolist()
    return jnp.split(t, offs, axis=-1)


def rms_norm(x, gain):
    xf = x.astype(jnp.float32)
    y = xf * lax.rsqrt(jnp.mean(xf * xf, axis=-1, keepdims=True) + RMS_EPS)
    return (y * gain.astype(jnp.float32)).astype(x.dtype)


def to_heads(t):
    return t.reshape(t.shape[0], t.shape[1], N_HEADS_GROUP, HEAD_DIM)


def to_chunks(t):
    b, s, _ = t.shape
    return t.reshape(b, s // CHUNK, CHUNK, N_HEADS_GROUP, HEAD_DIM).transpose(1, 0, 3, 2, 4)


def from_chunks(t):
    nc, b = t.shape[0], t.shape[1]
    return t.transpose(1, 0, 3, 2, 4).reshape(b, nc * CHUNK, GW)


def chunk_attention(q, k, v, rel_bias):
    b, s, _ = q.shape
    nc = s // CHUNK
    shp = (b, nc, CHUNK, N_HEADS_GROUP, HEAD_DIM)
    qc, kc, vc = q.reshape(shp), k.reshape(shp), v.reshape(shp)
    pad = ((0, 0), (ATTN_LEFT_CHUNKS, 0), (0, 0), (0, 0), (0, 0))
    band = jnp.arange(nc)[:, None] + jnp.arange(N_BAND)[None, :]
    kb = jnp.pad(kc, pad)[:, band].reshape(b, nc, N_BAND * CHUNK, N_HEADS_GROUP, HEAD_DIM)
    vb = jnp.pad(vc, pad)[:, band].reshape(b, nc, N_BAND * CHUNK, N_HEADS_GROUP, HEAD_DIM)
    scores = jnp.einsum('bcqhd,bckhd->bchqk', qc, kb) * ATTN_SCALE
    key_off = ((jnp.arange(N_BAND) - ATTN_LEFT_CHUNKS)[:, None] * CHUNK
               + jnp.arange(CHUNK)[None, :]).reshape(-1)
    rel = key_off[None, :] - jnp.arange(CHUNK)[:, None]
    bias = rel_bias.astype(jnp.float32)[:, jnp.clip(rel, -REL_CLIP, REL_CLIP) + REL_CLIP]
    valid = jnp.repeat(band >= ATTN_LEFT_CHUNKS, CHUNK, axis=1)
    scores = jnp.where(valid[None, :, None, None, :], scores + bias[None, None], NEG_INF)
    probs = jax.nn.softmax(scores, axis=-1)
    out = jnp.einsum('bchqk,bckhd->bcqhd', probs, vb)
    return out.reshape(b, s, GW)


def hgrn2(q_raw, f_raw, i_raw, g_raw, lb, norm_gain):
    b = q_raw.shape[0]
    log_f = jnp.logaddexp(jnp.log(lb), jnp.log1p(-lb) + jax.nn.log_sigmoid(f_raw))
    key = (1.0 - lb) * jax.nn.sigmoid(-f_raw)
    q = jax.nn.silu(q_raw)
    tri = jnp.tril(jnp.ones((CHUNK, CHUNK), dtype=bool))

    def step(state, inp):
        qc, kc, vc, gc = inp
        cum = jnp.cumsum(gc, axis=2)
        last = cum[:, :, -1:, :]
        o_inter = jnp.einsum('bhtk,bhkv->bhtv', qc * jnp.exp(cum), state)
        diff = cum[:, :, :, None, :] - cum[:, :, None, :, :]
        decay = jnp.exp(jnp.where(tri[None, None, :, :, None], diff, -jnp.inf))
        att = jnp.einsum('bhtk,bhsk,bhtsk->bhts', qc, kc, decay)
        o_intra = jnp.einsum('bhts,bhsv->bhtv', att, vc)
        new_state = (jnp.exp(last).transpose(0, 1, 3, 2) * state
                     + jnp.einsum('bhsk,bhsv->bhkv', kc * jnp.exp(last - cum), vc))
        return new_state, o_inter + o_intra

    state0 = jnp.zeros((b, N_HEADS_GROUP, HEAD_DIM, HEAD_DIM), jnp.float32)
    _, o = lax.scan(step, state0, (to_chunks(q), to_chunks(key), to_chunks(i_raw), to_chunks(log_f)))
    o = to_heads(from_chunks(o))
    o = o * lax.rsqrt(jnp.mean(o * o, axis=-1, keepdims=True) + RMS_EPS)
    return o.reshape(g_raw.shape) * norm_gain * jax.nn.silu(g_raw)


def rwkv7(pc, mu, w0, w2, a0, a2, g2, k_k, k_a, r_k, ln_w, ln_b):
    b, s, _ = pc.shape
    prev = jnp.pad(pc, ((0, 0), (1, 0), (0, 0)))[:, :-1]
    xs = pc + mu * (prev - pc)
    r, k, v, w_lo, a_lo, g_lo = split_cols(xs, [GW, GW, GW, RWKV_W_RANK, RWKV_A_RANK, RWKV_G_RANK])
    w_pre = -jax.nn.softplus(-(w0 + jnp.tanh(w_lo) @ w2)) - 0.5
    decay = jnp.exp(-jnp.exp(w_pre))
    a = jax.nn.sigmoid(a0 + a_lo @ a2)
    g = jax.nn.sigmoid(g_lo) @ g2
    kk = to_heads(k * k_k)
    kk = kk / jnp.maximum(jnp.sqrt(jnp.sum(kk * kk, axis=-1, keepdims=True)), 1e-12)
    k = k * (1.0 + (a - 1.0) * k_a)
    r_h, w_h, k_h, v_h, a_h = to_heads(r), to_heads(decay), to_heads(k), to_heads(v), to_heads(a)

    def step(state, inp):
        rt, wt, kt, vt, kkt, at = inp
        sa = jnp.einsum('bhvk,bhk->bhv', state, -kkt)
        state = (state * wt[:, :, None, :] + sa[..., None] * (kkt * at)[:, :, None, :]
                 + vt[..., None] * kt[:, :, None, :])
        return state, jnp.einsum('bhvk,bhk->bhv', state, rt)

    seq_first = lambda t: t.transpose(1, 0, 2, 3)
    state0 = jnp.zeros((b, N_HEADS_GROUP, HEAD_DIM, HEAD_DIM), jnp.float32)
    _, y = lax.scan(step, state0, (seq_first(r_h), seq_first(w_h), seq_first(k_h),
                                   seq_first(v_h), seq_first(kk), seq_first(a_h)))
    y = seq_first(y)
    mean = jnp.mean(y, axis=-1, keepdims=True)
    var = jnp.mean(jnp.square(y - mean), axis=-1, keepdims=True)
    y = ((y - mean) * lax.rsqrt(var + RWKV_GN_EPS)).reshape(b, s, GW) * ln_w + ln_b
    bonus = (jnp.sum(r_h * k_h * r_k, axis=-1, keepdims=True) * v_h).reshape(b, s, GW)
    return (y + bonus) * g


def rglru(xb, gb, conv_w, conv_b, wa, ba, wx, bx, lam):
    b, s, _ = xb.shape
    conv = lax.conv_general_dilated(xb, conv_w.astype(xb.dtype)[:, None, :], window_strides=(1,),
                                    padding=[(LRU_CONV - 1, 0)],
                                    dimension_numbers=('NWC', 'WIO', 'NWC'),
                                    feature_group_count=GW) + conv_b
    xh = conv.reshape(b, s, LRU_BLOCKS, LRU_BLOCK_DIM)
    gate_r = jax.nn.sigmoid(jnp.einsum('bsnd,nde->bsne', xh, wa).reshape(b, s, GW) + ba)
    gate_i = jax.nn.sigmoid(jnp.einsum('bsnd,nde->bsne', xh, wx).reshape(b, s, GW) + bx)
    log_a = -LRU_C * gate_r * jax.nn.softplus(-lam)
    a = jnp.exp(log_a)
    inp = jnp.sqrt(-jnp.expm1(2.0 * log_a)) * (gate_i * conv)

    def combine(c1, c2):
        a1, b1 = c1
        a2, b2 = c2
        return a1 * a2, a2 * b1 + b2

    _, h = lax.associative_scan(combine, (a, inp), axis=1)
    return h * jax.nn.gelu(gb)


def setup_inputs(seed: int = 0) -> dict:
    key = jax.random.key(seed)
    ks = jax.random.split(key, 32)
    L = DEPTH

    def nrm(k, shape, scale):
        return jax.random.normal(k, shape, jnp.float32) * scale

    u = jax.random.uniform(ks[27], (L, GW), jnp.float32, 0.9, 0.999)
    a_root = u ** (1.0 / LRU_C)
    lam = jnp.log(a_root) - jnp.log1p(-a_root)
    return {
        'x': nrm(ks[0], (BATCH, SEQ, D_MODEL), 1.0),
        'norm_mix_pre': 1.0 + nrm(ks[1], (L, D_MODEL), 0.05),
        'norm_mix_post': 1.0 + nrm(ks[2], (L, D_MODEL), 0.05),
        'norm_mlp_pre': 1.0 + nrm(ks[3], (L, D_MODEL), 0.05),
        'norm_mlp_post': 1.0 + nrm(ks[4], (L, D_MODEL), 0.05),
        'w_in': nrm(ks[5], (L, D_MODEL, D_IN), D_MODEL ** -0.5),
        'w_out': nrm(ks[6], (L, D_MIX, D_MODEL), D_MIX ** -0.5),
        'attn_rel_bias': nrm(ks[7], (L, N_HEADS_GROUP, 2 * REL_CLIP + 1), 0.2),
        'hgrn_lb_logits': nrm(ks[8], (L, GW), 0.5),
        'hgrn_norm': 1.0 + nrm(ks[9], (L, GW), 0.05),
        'rwkv_mu': jax.random.uniform(ks[10], (L, RWKV_COLS), jnp.float32, 0.2, 0.8),
        'rwkv_w0': jax.random.uniform(ks[11], (L, GW), jnp.float32, -6.0, 1.0),
        'rwkv_w2': nrm(ks[12], (L, RWKV_W_RANK, GW), 0.5 * RWKV_W_RANK ** -0.5),
        'rwkv_a0': nrm(ks[13], (L, GW), 0.1),
        'rwkv_a2': nrm(ks[14], (L, RWKV_A_RANK, GW), 0.5 * RWKV_A_RANK ** -0.5),
        'rwkv_g2': nrm(ks[15], (L, RWKV_G_RANK, GW), RWKV_G_RANK ** -0.5),
        'rwkv_k_k': 0.85 + nrm(ks[16], (L, GW), 0.05),
        'rwkv_k_a': 1.0 + nrm(ks[17], (L, GW), 0.05),
        'rwkv_r_k': nrm(ks[18], (L, N_HEADS_GROUP, HEAD_DIM), 0.1),
        'rwkv_ln_w': 1.0 + nrm(ks[19], (L, GW), 0.05),
        'rwkv_ln_b': nrm(ks[20], (L, GW), 0.02),
        'lru_conv_w': nrm(ks[21], (L, LRU_CONV, GW), 0.5 * LRU_CONV ** -0.5),
        'lru_conv_b': nrm(ks[22], (L, GW), 0.02),
        'lru_wa': nrm(ks[23], (L, LRU_BLOCKS, LRU_BLOCK_DIM, LRU_BLOCK_DIM), LRU_BLOCK_DIM ** -0.5),
        'lru_ba': nrm(ks[24], (L, GW), 0.02),
        'lru_wx': nrm(ks[25], (L, LRU_BLOCKS, LRU_BLOCK_DIM, LRU_BLOCK_DIM), LRU_BLOCK_DIM ** -0.5),
        'lru_bx': nrm(ks[26], (L, GW), 0.02),
        'lru_lambda': lam,
        'mlp_w1': nrm(ks[28], (L, D_MODEL, MLP_HIDDEN), D_MODEL ** -0.5),
        'mlp_w2': nrm(ks[29], (L, MLP_HIDDEN, D_MODEL), MLP_HIDDEN ** -0.5),
    }


def reference(x, norm_mix_pre, norm_mix_post, norm_mlp_pre, norm_mlp_post, w_in, w_out,
              attn_rel_bias, hgrn_lb_logits, hgrn_norm, rwkv_mu, rwkv_w0, rwkv_w2, rwkv_a0, rwkv_a2,
              rwkv_g2, rwkv_k_k, rwkv_k_a, rwkv_r_k, rwkv_ln_w, rwkv_ln_b, lru_conv_w, lru_conv_b,
              lru_wa, lru_ba, lru_wx, lru_bx, lru_lambda, mlp_w1, mlp_w2):
    dt = x.dtype
    f32 = jnp.float32
    lb_sm = jax.nn.softmax(hgrn_lb_logits.astype(f32), axis=0)
    lb_all = jnp.maximum(jnp.cumsum(lb_sm, axis=0) - lb_sm[0:1], 0.0)
    for l in range(DEPTH):
        h = rms_norm(x, norm_mix_pre[l])
        proj = (h @ w_in[l]).astype(f32)
        pa, pb, pc, pd = split_cols(proj, [ATTN_COLS, HGRN_COLS, RWKV_COLS, LRU_COLS])
        qa, ka, va = split_cols(pa, [GW, GW, GW])
        ya = chunk_attention(qa, ka, va, attn_rel_bias[l])
        qb, fb, ib, gb = split_cols(pb, [GW, GW, GW, GW])
        yb = hgrn2(qb, fb, ib, gb, lb_all[l], hgrn_norm[l].astype(f32))
        yc = rwkv7(pc, rwkv_mu[l].astype(f32), rwkv_w0[l].astype(f32), rwkv_w2[l].astype(f32),
                   rwkv_a0[l].astype(f32), rwkv_a2[l].astype(f32), rwkv_g2[l].astype(f32),
                   rwkv_k_k[l].astype(f32), rwkv_k_a[l].astype(f32), rwkv_r_k[l].astype(f32),
                   rwkv_ln_w[l].astype(f32), rwkv_ln_b[l].astype(f32))
        xd, gd = split_cols(pd, [GW, GW])
        yd = rglru(xd, gd, lru_conv_w[l].astype(f32), lru_conv_b[l].astype(f32), lru_wa[l].astype(f32),
                   lru_ba[l].astype(f32), lru_wx[l].astype(f32), lru_bx[l].astype(f32),
                   lru_lambda[l].astype(f32))
        mix = jnp.concatenate([ya, yb, yc, yd], axis=-1).astype(dt)
        x = x + rms_norm(mix @ w_out[l], norm_mix_post[l])
        h = rms_norm(x, norm_mlp_pre[l])
        ff = jnp.square(jax.nn.relu(h @ mlp_w1[l])) @ mlp_w2[l]
        x = x + rms_norm(ff, norm_mlp_post[l])
    return x
```

```python
import os
import numpy as np
from contextlib import ExitStack
import concourse.bass as bass
import concourse.mybir as mybir
from concourse.bass_utils import run_bass_kernel_spmd

F32 = mybir.dt.float32
BF16 = mybir.dt.bfloat16
AF = mybir.ActivationFunctionType
ALU = mybir.AluOpType
AX = mybir.AxisListType

D = 1024
DIN = 3200
HID = 4096
TB = 256
CH = 64
NCK = TB // CH
ENGS = ("pe", "act", "dve", "pool", "sp")

PVL = {}


def _pv_layout():
    off = 0
    for name, n in [("nmixpre", 8), ("nmixpost", 8), ("nmlppre", 8), ("nmlppost", 8), ("lblog", 2), ("hnorm", 2),
                    ("mu", 7), ("w0", 2), ("a0", 2), ("kk", 2), ("ka", 2), ("rk", 2), ("lnw", 2), ("lnb", 2),
                    ("cw0", 2), ("cw1", 2), ("cw2", 2), ("cw3", 2), ("cb", 2), ("ba", 2), ("bx", 2), ("lam", 2)]:
        PVL[name] = (off, n)
        off += n
    return off


NPV = _pv_layout()

CL = {}


def _c_layout():
    off = 0
    for name, n in [("I", 128), ("E2", 64), ("su", 128), ("nsu", 128), ("nsl", 128), ("iu", 64), ("niu", 64),
                    ("cm", TB), ("bd1", 128), ("bd64", 128)]:
        CL[name] = (off, n)
        off += n
    return off


NCONST = _c_layout()


def make_consts():
    c = np.zeros((128, NCONST), np.float32)
    p = np.arange(128)

    def put(name, arr):
        o, n = CL[name]
        c[:, o:o + n] = arr

    put("I", np.eye(128, dtype=np.float32))
    put("E2", np.concatenate([np.eye(64), np.eye(64)], 0))
    blk = (p[:, None] // 64) == (p[None, :] // 64)
    s = p[:, None] % 64
    t = p[None, :] % 64
    su = (blk & (s < t)).astype(np.float32)
    put("su", su)
    put("nsu", -su)
    put("nsl", -(blk & (s > t)).astype(np.float32))
    iu = ((p[:, None] % 64) <= np.arange(64)[None, :]).astype(np.float32)
    put("iu", iu)
    put("niu", -iu)
    cm = np.ones((128, TB), np.float32)
    cm[:, ::CH] = 0.0
    put("cm", cm)
    put("bd1", blk.astype(np.float32))
    put("bd64", blk.astype(np.float32) / 64.0)
    return c


class Sched:
    NDMA = 24

    def __init__(self, nc, sems_c, sems_d, sems_sw, swscratch):
        self.nc = nc
        self.alias = {}
        self.sw = sems_sw
        self.swscratch = swscratch
        self.sc = sems_c
        self.sd = sems_d
        self.pending = []
        self.last_w = {}
        self.readers = {}
        self.cnt = {e: 0 for e in ENGS}
        self.known = {e: {} for e in ENGS}
        self.dma_sigs = []
        self.nops = 0
        self.nwait = 0

    def eng(self, name):
        nc = self.nc
        return {"pe": nc.tensor, "act": nc.scalar, "dve": nc.vector, "pool": nc.gpsimd, "sp": nc.sync}[name]

    def _exp(self, keys):
        out = []
        for k in keys:
            if k in self.alias:
                out.extend(self.alias[k])
            else:
                out.append(k)
        return tuple(out)

    def add(self, eng, fn, reads=(), writes=(), dma=False):
        reads = self._exp(reads)
        writes = self._exp(writes)
        self.pending.append(dict(eng=eng, fn=fn, reads=tuple(reads), writes=tuple(writes), dma=dma, sig=None,
                                 force=False))

    def dma(self, eng, out, in_, reads=(), writes=()):
        self.add(eng, lambda e: e.dma_start(out=out, in_=in_), reads, writes, dma=True)

    def flush(self):
        ops = self.pending
        self.pending = []
        for op in ops:
            d = []
            raw = []
            for k in op["reads"]:
                w = self.last_w.get(k)
                if w is not None:
                    d.append(w)
                    raw.append(w)
            for k in op["writes"]:
                w = self.last_w.get(k)
                if w is not None:
                    d.append(w)
                for r in self.readers.get(k, ()):
                    d.append(r)
            dd = []
            seen = set()
            for oj in d:
                if oj is op or id(oj) in seen:
                    continue
                seen.add(id(oj))
                if (not oj["dma"]) and oj["eng"] == op["eng"] and not op["dma"]:
                    if op["eng"] == "pe":
                        continue
                    pass
                dd.append(oj)
            op["deps"] = dd
            for oj in dd:
                oj["force"] = True
            for k in op["reads"]:
                self.readers.setdefault(k, []).append(op)
            for k in op["writes"]:
                self.last_w[k] = op
                self.readers[k] = []
        for k, w in self.last_w.items():
            w["force"] = True
        for k, rs in self.readers.items():
            for r in rs:
                r["force"] = True
        sw_run = []
        for idx, op in enumerate(ops):
            e = op["eng"]
            engine = self.eng(e)
            need = {}
            swdma = op["dma"] and e == "pool"
            if swdma:
                pass
            elif op["dma"]:
                di = len(self.dma_sigs)
                sk = ("dma", di % self.NDMA)
                v = 16 * (di // self.NDMA + 1)
                op["sig"] = (sk, v)
                if di >= self.NDMA:
                    psk, pv = self.dma_sigs[di - self.NDMA]
                    need[psk] = pv
                self.dma_sigs.append((sk, v))
            elif op["force"]:
                self.cnt[e] += 1
                op["sig"] = ((e,), self.cnt[e])
            for oj in op["deps"]:
                if oj["sig"] is None and swdma:
                    continue
                sk, v = oj["sig"]
                if need.get(sk, 0) < v:
                    need[sk] = v
            for sk, v in need.items():
                if self.known[e].get(sk, 0) >= v:
                    continue
                sem = self.sd[sk[1]] if sk[0] == "dma" else self.sc[sk[0]]
                engine.wait_ge(sem, v)
                self.known[e][sk] = v
                self.nwait += 1
            ins = op["fn"](engine)
            self.nops += 1
            if swdma:
                sem = self.sw[len(sw_run)]
                ins.then_inc(sem, 16)
                sw_run.append((op, sem))
                nxt = ops[idx + 1] if idx + 1 < len(ops) else None
                if len(sw_run) == len(self.sw) or nxt is None or not (nxt["dma"] and nxt["eng"] == "pool"):
                    for (o2, sem2) in sw_run:
                        engine.wait_ge(sem2, 16)
                        engine.sem_clear(sem2)
                    self.cnt["pool"] += 1
                    engine.memset(self.swscratch, 0.0).then_inc(self.sc["pool"], 1)
                    for (o2, sem2) in sw_run:
                        o2["sig"] = (("pool",), self.cnt["pool"])
                    sw_run = []
            elif op["sig"] is not None:
                sk, v = op["sig"]
                if sk[0] == "dma":
                    ins.then_inc(self.sd[sk[1]], 16)
                else:
                    ins.then_inc(self.sc[sk[0]], 1)
            op["fn"] = None
            op["deps"] = None

    def finish(self, eng="sp"):
        self.flush()
        engine = self.eng(eng)
        tot = {}
        for sk, v in self.dma_sigs:
            tot[sk] = max(tot.get(sk, 0), v)
        for sk, v in tot.items():
            engine.wait_ge(self.sd[sk[1]], v)


def build(S_LEN, L, debug_mix=False, do_mix=True, do_mlp=True):
    NB = S_LEN // TB
    nc = bass.Bass("TRN2", target_bir_lowering=False)

    def din(name, shape):
        return nc.dram_tensor(name, list(shape), F32, kind="ExternalInput").ap()

    xT_in = din("xT", [D, S_LEN])
    w_in = din("w_in", [L, D, DIN])
    w_out = din("w_out", [L, D, D])
    w1d = din("mlp_w1", [L, D, HID])
    w2d = din("mlp_w2", [L, HID, D])
    pvec_d = din("pvec", [128, L, NPV])
    lr3_d = din("lr3", [128, L, 3, 256])
    lruw_d = din("lruw", [128, L, 2, 2, 128])
    bias_d = din("biasT", [128, L, 5, 4, 128])
    const_d = din("consts", [128, NCONST])
    yT_out = nc.dram_tensor("yT", [D, S_LEN], F32, kind="ExternalOutput").ap()
    scr = [nc.dram_tensor("scr%d" % i, [D, S_LEN], F32, kind="Internal").ap() for i in range(2)]
    if debug_mix:
        mix_dbg = nc.dram_tensor("mixdbg", [D, S_LEN], F32, kind="ExternalOutput").ap()

    def dview(ap, b):
        return ap.rearrange("(k p) s -> p k s", p=128)[:, :, b * TB:(b + 1) * TB]

    with ExitStack() as es0:
        sems_c = {e: es0.enter_context(nc.semaphore("s_" + e)) for e in ENGS}
        sems_d = [es0.enter_context(nc.semaphore("d_%d" % i)) for i in range(Sched.NDMA)]
        sems_sw = [es0.enter_context(nc.semaphore("w_%d" % i)) for i in range(8)]
        swscr = es0.enter_context(nc.sbuf_tensor("swscr", [128, 2], F32))
        S = Sched(nc, sems_c, sems_d, sems_sw, swscr[:, 0:1])

        _uid = [0]

        def sbg(es, name, shape, dt):
            _uid[0] += 1
            return es.enter_context(nc.sbuf_tensor("%s_u%d" % (name, _uid[0]), list(shape), dt))

        PS = [es0.enter_context(nc.psum_tensor("psb%d" % i, [128, 512], F32)) for i in range(8)]

        def half(i):
            return PS[i % 8][:, 0:256], "ps%d" % (i % 8)

        def quarter(i):
            return PS[i // 4][:, (i % 4) * 128:(i % 4) * 128 + 128], "pq%d" % i

        consts = sbg(es0, "consts", [128, NCONST], F32)
        pvec = sbg(es0, "pvec", [128, L, NPV], F32)
        onesD = sbg(es0, "onesD", [128, 128], BF16)
        ones64 = sbg(es0, "ones64", [128, 64], BF16)
        lbv = sbg(es0, "lbv", [128, L, 2], F32)
        omlb = sbg(es0, "omlb", [128, L, 2], F32)
        nsp8 = sbg(es0, "nsp8", [128, L, 2], F32)
        tmpv = sbg(es0, "tmpv", [128, L, 2], F32)
        tmpv2 = sbg(es0, "tmpv2", [128, 2], F32)
        epsc = sbg(es0, "epsc", [128, 4], F32)

        constsb = sbg(es0, "constsb", [128, NCONST], BF16)

        def C(name):
            o, n = CL[name]
            return consts[:, o:o + n]

        def Cb(name):
            o, n = CL[name]
            return constsb[:, o:o + n]

        def pv(l, name, j=0):
            o, n = PVL[name]
            return pvec[:, l, o + j:o + j + 1]

        S.dma("sp", consts[:], const_d, writes=["consts"])
        S.dma("sp", pvec[:], pvec_d, writes=["pvec"])
        S.add("dve", lambda e: e.tensor_copy(out=constsb[:], in_=consts[:]), ["consts"], ["constsb"])
        S.add("pool", lambda e: e.memset(onesD[:], 1.0 / D), [], ["onesD"])
        S.add("pool", lambda e: e.memset(ones64[:], 1.0), [], ["ones64"])
        S.add("pool", lambda e: e.memset(epsc[:, 0:1], 1e-6), [], ["epsc"])
        S.add("pool", lambda e: e.memset(epsc[:, 1:2], 64e-5), [], ["epsc"])
        S.add("pool", lambda e: e.memset(epsc[:, 2:3], 1e-24), [], ["epsc"])
        S.add("pool", lambda e: e.memset(epsc[:, 3:4], -1.0), [], ["epsc"])
        o_lb = PVL["lblog"][0]
        o_lam = PVL["lam"][0]
        S.add("act", lambda e: e.activation(out=tmpv[:], in_=pvec[:, :, o_lb:o_lb + 2], func=AF.Exp), ["pvec"], ["tmpv"])
        S.add("dve", lambda e: e.tensor_reduce(out=tmpv2[:], in_=tmpv[:].rearrange("p l c -> p c l"), axis=AX.X,
                                                op=ALU.add), ["tmpv"], ["tmpv2"])
        S.add("dve", lambda e: e.reciprocal(out=tmpv2[:], in_=tmpv2[:]), ["tmpv2"], ["tmpv2"])
        S.add("dve", lambda e: e.tensor_tensor(out=tmpv[:], in0=tmpv[:], in1=tmpv2[:].unsqueeze(1).to_broadcast([128, L, 2]),
                                                op=ALU.mult), ["tmpv", "tmpv2"], ["tmpv"])
        S.add("dve", lambda e: e.memset(lbv[:, 0, :], 0.0), [], ["lbv"])
        for l in range(1, L):
            S.add("dve", (lambda l: lambda e: e.tensor_tensor(out=lbv[:, l, :], in0=lbv[:, l - 1, :], in1=tmpv[:, l, :],
                                                              op=ALU.add))(l), ["lbv", "tmpv"], ["lbv"])
        S.add("dve", lambda e: e.tensor_scalar(out=omlb[:], in0=lbv[:], scalar1=-1.0, scalar2=1.0, op0=ALU.mult,
                                               op1=ALU.add), ["lbv"], ["omlb"])
        S.add("act", lambda e: e.activation(out=nsp8[:], in_=pvec[:, :, o_lam:o_lam + 2], func=AF.Exp, scale=-1.0),
              ["pvec"], ["nsp8"])
        S.add("act", lambda e: e.activation(out=nsp8[:], in_=nsp8[:], func=AF.Ln, bias=1.0), ["nsp8"], ["nsp8"])
        S.add("dve", lambda e: e.tensor_scalar(out=nsp8[:], in0=nsp8[:], scalar1=-8.0, scalar2=None, op0=ALU.mult),
              ["nsp8"], ["nsp8"])
        S.flush()

        def mm(out, outk, lhsT, rhs, reads, start=True, stop=True):
            S.add("pe", lambda e: e.matmul(out, lhsT=lhsT, rhs=rhs, start=start, stop=stop), reads, [outk])

        def load_cast(dst, dstk, src, stg, stgk, n, i, view=None):
            sv = stg[:, 0:n]
            if view is not None:
                sv = view(sv)
            S.dma("sp", sv, src, writes=[stgk])
            eng = ("act", "dve", "pool")[i % 3]
            if eng == "act":
                S.add("act", lambda e: e.copy(out=dst, in_=sv), [stgk], [dstk])
            else:
                S.add(eng, lambda e: e.tensor_copy(out=dst, in_=sv), [stgk], [dstk])

        def rmsnorm(X, xk, hT, hk, rstd, rk, l, gname, slot):
            ps, pk = half(slot)
            S.add("act", lambda e: e.activation(out=hT[:], in_=X[:], func=AF.Square), [xk], [hk])
            for kc in range(8):
                mm(ps, pk, onesD[:], hT[:, kc, :], [hk, "onesD"], start=(kc == 0), stop=(kc == 7))
            S.add("act", lambda e: e.activation(out=rstd[:], in_=ps, func=AF.Sqrt, bias=epsc[:, 0:1]), [pk, "epsc"], [rk])
            S.add("dve", lambda e: e.reciprocal(out=rstd[:], in_=rstd[:]), [rk], [rk])
            for kc in range(8):
                S.add("dve", (lambda kc: lambda e: e.scalar_tensor_tensor(out=hT[:, kc, :], in0=X[:, kc, :],
                                                                          scalar=pv(l, gname, kc), in1=rstd[:],
                                                                          op0=ALU.mult, op1=ALU.mult))(kc),
                      [xk, rk, "pvec"], [hk])

        def postnorm_residual(Y, yk, X, xk, sq, sqk, rstd, rk, l, gname, slot):
            ps, pk = half(slot)
            S.add("act", lambda e: e.activation(out=sq[:], in_=Y[:], func=AF.Square), [yk], [sqk])
            for kc in range(8):
                mm(ps, pk, onesD[:], sq[:, kc, :], [sqk, "onesD"], start=(kc == 0), stop=(kc == 7))
            S.add("act", lambda e: e.activation(out=rstd[:], in_=ps, func=AF.Sqrt, bias=epsc[:, 0:1]), [pk, "epsc"], [rk])
            S.add("dve", lambda e: e.reciprocal(out=rstd[:], in_=rstd[:]), [rk], [rk])
            S.add("pool", lambda e: e.tensor_tensor(out=Y[:], in0=Y[:], in1=rstd[:].unsqueeze(1).to_broadcast([128, 8, TB]),
                                                    op=ALU.mult), [yk, rk], [yk])
            for kc in range(8):
                S.add("dve", (lambda kc: lambda e: e.scalar_tensor_tensor(out=Y[:, kc, :], in0=Y[:, kc, :],
                                                                          scalar=pv(l, gname, kc), in1=X[:, kc, :],
                                                                          op0=ALU.mult, op1=ALU.add))(kc),
                      [yk, xk, "pvec"], [yk])

        def phase_mlp(l, src, srck, dst, dstk):
            with ExitStack() as es:
                w1 = sbg(es, "w1", [128, 8, HID], BF16)
                w2 = sbg(es, "w2", [128, 32, D], BF16)
                xbs = [sbg(es, "xbB%d" % i, [128, 8, TB], F32) for i in range(2)]
                hTs = [sbg(es, "hTB%d" % i, [128, 8, TB], BF16) for i in range(2)]
                sqs = sbg(es, "sqsB", [128, 8, TB], BF16)
                hid = sbg(es, "hid", [128, 32, TB], BF16)
                yTs = [sbg(es, "yTB%d" % i, [128, 8, TB], F32) for i in range(2)]
                rstd = sbg(es, "rstdB", [128, TB], F32)
                rstd2 = sbg(es, "rstdB2", [128, TB], F32)
                sqt = [sbg(es, "sqtB%d" % i, [128, TB], F32) for i in range(2)]
                stg = sbg(es, "stgB", [128, 1024], F32)
                for kc in range(8):
                    for hf in range(4):
                        load_cast(w1[:, kc, hf * 1024:(hf + 1) * 1024], ("w1", kc),
                                  w1d[l, kc * 128:(kc + 1) * 128, hf * 1024:(hf + 1) * 1024], stg, "stgB", 1024, 4 * kc + hf)
                for g in range(32):
                    load_cast(w2[:, g, :], ("w2", g // 4), w2d[l, 128 * g:128 * (g + 1), :], stg, "stgB", 1024, g)
                cnt = 0

                def prep(b):
                    X = xbs[b % 2]
                    xk = "xbB%d" % (b % 2)
                    S.dma("sp", X[:], dview(src, b), reads=[(srck, b)], writes=[xk])
                    rmsnorm(X, xk, hTs[b % 2], "hTB%d" % (b % 2), rstd, "rstdB", l, "nmlppre", 4 + (b % 2))

                prep(0)
                for b in range(NB):
                    X = xbs[b % 2]
                    xk = "xbB%d" % (b % 2)
                    Y = yTs[b % 2]
                    yk = "yTB%d" % (b % 2)
                    hT = hTs[b % 2]
                    hk = "hTB%d" % (b % 2)
                    for hc in range(32):
                        ps, pk = half(cnt % 4)
                        sq = sqt[cnt % 2]
                        sqk = "sqtB%d" % (cnt % 2)
                        cnt += 1
                        for kc in range(8):
                            mm(ps, pk, w1[:, kc, hc * 128:(hc + 1) * 128], hT[:, kc, :], [("w1", kc), hk], start=(kc == 0),
                               stop=(kc == 7))
                        S.add("act", (lambda ps, sq: lambda e: e.activation(out=sq[:], in_=ps, func=AF.Relu))(ps, sq),
                              [pk], [sqk])
                        S.add("dve", (lambda sq, hc: lambda e: e.tensor_tensor(
                            out=hid[:, hc, :], in0=sq[:], in1=sq[:], op=ALU.mult))(sq, hc),
                            [sqk], [("hid", hc)])
                    if b + 1 < NB:
                        prep(b + 1)
                    for oc in range(8):
                        ps, pk = half(cnt % 4)
                        cnt += 1
                        for kc in range(32):
                            mm(ps, pk, w2[:, kc, oc * 128:(oc + 1) * 128], hid[:, kc, :], [("w2", kc // 4), ("hid", kc)],
                               start=(kc == 0), stop=(kc == 31))
                        S.add("act", (lambda ps, oc, Y: lambda e: e.copy(out=Y[:, oc, :], in_=ps))(ps, oc, Y), [pk], [yk])
                    postnorm_residual(Y, yk, X, xk, sqs, "sqsB", rstd2, "rstdB2", l, "nmlppost", 6 + (b % 2))
                    S.dma("sp", dview(dst, b), Y[:], reads=[yk], writes=[(dstk, b)])
                S.flush()

        def phase_mix(l, src, srck, dst, dstk):
            from_mix = _phase_mix_impl
            from_mix(l, src, srck, dst, dstk)

        MIXERS = os.environ.get("KMIX", "ALHR")
        KR = int(os.environ.get("KR", "0"))

        def _phase_mix_impl(l, src, srck, dst, dstk):
            with ExitStack() as es:
                win = sbg(es, "win", [128, 8, DIN], BF16)
                wout = sbg(es, "wout", [128, 8, D], BF16)
                stg = sbg(es, "stgA", [128, DIN // 4], F32)
                for kc in range(8):
                    for hf in range(4):
                        cs_ = slice(hf * (DIN // 4), (hf + 1) * (DIN // 4))
                        load_cast(win[:, kc, cs_], ("win", kc), w_in[l, kc * 128:(kc + 1) * 128, cs_], stg, "stgA",
                                  DIN // 4, 4 * kc + hf)
                for kc in range(8):
                    for hf in range(2):
                        load_cast(wout[:, kc, hf * 512:(hf + 1) * 512], ("wout", kc),
                                  w_out[l, kc * 128:(kc + 1) * 128, hf * 512:(hf + 1) * 512], stg, "stgA", 512, 2 * kc + hf)
                biasT = sbg(es, "biasT", [128, 5, 4, 128], F32)
                S.dma("sp", biasT[:], bias_d[:, l], writes=["biasT"])
                lr3 = sbg(es, "lr3", [128, 3, 256], F32)
                S.dma("sp", lr3[:], lr3_d[:, l], writes=["lr3"])
                lruw = sbg(es, "lruw", [128, 2, 2, 128], F32)
                S.dma("sp", lruw[:], lruw_d[:, l], writes=["lruw"])
                xb = sbg(es, "xbA", [128, 8, TB], F32)
                hT = sbg(es, "hTA", [128, 8, TB], BF16)
                rstd = sbg(es, "rstdA", [128, TB], F32)
                mixT = sbg(es, "mixT", [128, 8, TB], BF16)
                Gbig = sbg(es, "Gbig", [128, 14, 2, TB], F32)
                G = [Gbig[:, i] for i in range(14)]
                Gk = ["G%d" % i for i in range(14)]
                yT = Gbig[:, 10:14].rearrange("p a b t -> p (a b) t")
                S.alias["yTA"] = [Gk[10], Gk[11], Gk[12], Gk[13]]
                xe = sbg(es, "xe", [128, 2, 3 + TB], F32)
                hprev = sbg(es, "hprev", [128, 2], F32)
                S.add("pool", lambda e: e.memset(xe[:], 0.0), [], ["xe"])
                S.add("pool", lambda e: e.memset(hprev[:], 0.0), [], ["hprev"])
                S.add("pool", lambda e: e.memset(mixT[:], 0.0), [], ["mixT"])
                kz = sbg(es, "kz", [128, 2, 2, 8, 128], BF16)
                vring = sbg(es, "vring", [128, 8, 256], BF16)
                qT = sbg(es, "qT", [128, 2, TB], BF16)
                s_sb = sbg(es, "s_sb", [128, 4, 128], F32)
                pT = sbg(es, "pT", [128, 5, 4, 128], BF16)
                rden = sbg(es, "rden", [128, 2, 128], F32)
                S.add("pool", lambda e: e.memset(kz[:], 0.0), [], ["kz"])
                S.add("pool", lambda e: e.memset(vring[:], 0.0), [], ["vring"])
                dcnt = [0]

                def inproj(c, dst_ap, dstk):
                    ps, pk = half(dcnt[0] % 2)
                    dcnt[0] += 1
                    for kc in range(8):
                        mm(ps, pk, win[:, kc, c * 128:(c + 1) * 128], hT[:, kc, :], [("win", kc), "hTA"], start=(kc == 0),
                           stop=(kc == 7))
                    S.add("act", lambda e: e.copy(out=dst_ap, in_=ps), [pk], [dstk])

                def mixer_lru(b):
                    gb, cv, gr, gi, aa, hs = G[0], G[1], G[2], G[3], G[4], G[5]
                    for ch in range(2):
                        inproj(21 + ch, xe[:, ch, 3:3 + TB], "xe")
                        inproj(23 + ch, gb[:, ch, :], Gk[0])
                        yield
                    for ch in range(2):
                        S.add("dve", (lambda ch: lambda e: e.tensor_scalar(
                            out=cv[:, ch, :], in0=xe[:, ch, 0:TB], scalar1=pv(l, "cw0", ch), scalar2=pv(l, "cb", ch),
                            op0=ALU.mult, op1=ALU.add))(ch), ["xe", "pvec"], [Gk[1]])
                        for j in range(1, 4):
                            S.add("dve", (lambda ch, j: lambda e: e.scalar_tensor_tensor(
                                out=cv[:, ch, :], in0=xe[:, ch, j:j + TB], scalar=pv(l, "cw%d" % j, ch), in1=cv[:, ch, :],
                                op0=ALU.mult, op1=ALU.add))(ch, j), ["xe", "pvec", Gk[1]], [Gk[1]])
                            yield
                    S.add("pool", lambda e: e.tensor_copy(out=xe[:, :, 0:3], in_=xe[:, :, TB:TB + 3]), ["xe"], ["xe"])
                    for ch in range(2):
                        for i, (dstt, dk, bn) in enumerate([(gr, Gk[2], "ba"), (gi, Gk[3], "bx")]):
                            ps, pk = half(2)
                            mm(ps, pk, lruw[:, i, ch, :], cv[:, ch, :], ["lruw", Gk[1]])
                            S.add("act", (lambda ps, dstt, ch, bn: lambda e: e.activation(
                                out=dstt[:, ch, :], in_=ps, func=AF.Sigmoid, bias=pv(l, bn, ch)))(ps, dstt, ch, bn),
                                [pk, "pvec"], [dk])
                            yield
                        S.add("act", (lambda ch: lambda e: e.activation(out=aa[:, ch, :], in_=gr[:, ch, :], func=AF.Exp,
                                                                        scale=nsp8[:, l, ch:ch + 1]))(ch),
                              [Gk[2], "nsp8"], [Gk[4]])
                    S.add("dve", lambda e: e.tensor_tensor(out=gr[:], in0=aa[:], in1=aa[:], op=ALU.mult), [Gk[4]], [Gk[2]])
                    S.add("dve", lambda e: e.tensor_scalar(out=gr[:], in0=gr[:], scalar1=-1.0, scalar2=1.0, op0=ALU.mult,
                                                           op1=ALU.add), [Gk[2]], [Gk[2]])
                    S.add("dve", lambda e: e.tensor_scalar(out=gr[:], in0=gr[:], scalar1=1e-30, scalar2=None,
                                                           op0=ALU.max), [Gk[2]], [Gk[2]])
                    yield
                    S.add("act", lambda e: e.activation(out=gr[:], in_=gr[:], func=AF.Sqrt), [Gk[2]], [Gk[2]])
                    yield
                    S.add("dve", lambda e: e.tensor_tensor(out=gi[:], in0=gi[:], in1=cv[:], op=ALU.mult), [Gk[3], Gk[1]], [Gk[3]])
                    S.add("dve", lambda e: e.tensor_tensor(out=gi[:], in0=gi[:], in1=gr[:], op=ALU.mult), [Gk[3], Gk[2]], [Gk[3]])
                    for ch in range(2):
                        S.add("dve", (lambda ch: lambda e: e.tensor_tensor_scan(
                            out=hs[:, ch, :], data0=aa[:, ch, :], data1=gi[:, ch, :], initial=hprev[:, ch:ch + 1],
                            op0=ALU.mult, op1=ALU.add))(ch), [Gk[4], Gk[3], "hprev"], [Gk[5]])
                        yield
                    S.add("pool", lambda e: e.tensor_copy(out=hprev[:], in_=hs[:, :, TB - 1]), [Gk[5]], ["hprev"])
                    S.add("act", lambda e: e.activation(out=gb[:], in_=gb[:], func=AF.Gelu_apprx_tanh), [Gk[0]], [Gk[0]])
                    S.add("dve", lambda e: e.tensor_tensor(out=mixT[:, 6:8, :], in0=hs[:], in1=gb[:], op=ALU.mult),
                          [Gk[5], Gk[0]], ["mixT"])

                def mixer_attn(b):
                    q, k, v = G[6], G[7], G[8]
                    for pr in range(2):
                        inproj(0 + pr, q[:, pr, :], Gk[6])
                        inproj(2 + pr, k[:, pr, :], Gk[7])
                        inproj(4 + pr, v[:, pr, :], Gk[8])
                        yield
                    S.add("act", lambda e: e.copy(out=qT[:], in_=q[:]), [Gk[6]], ["qT"])
                    s0 = (2 * b) % 8
                    for pr in range(2):
                        for h2 in range(2):
                            S.add("pool", (lambda pr, h2: lambda e: e.tensor_copy(
                                out=kz[64 * h2:64 * h2 + 64, pr, h2, s0:s0 + 2, :],
                                in_=k[64 * h2:64 * h2 + 64, pr, :].rearrange("p (s t) -> p s t", s=2)))(pr, h2),
                                [Gk[7]], ["kz"])
                    for tt in range(2):
                        pst = PS[7]
                        for pr in range(2):
                            S.add("pe", (lambda tt, pr: lambda e: e.transpose(
                                out=pst[:, pr * 128:(pr + 1) * 128], in_=v[:, pr, tt * 128:(tt + 1) * 128],
                                identity=C("I")))(tt, pr), [Gk[8], "consts"], ["ps7"])
                        S.add("act", (lambda tt: lambda e: e.copy(out=vring[:, s0 + tt, :], in_=pst[:, 0:256]))(tt),
                              ["ps7"], ["vring"])
                        yield
                    for tt in range(2):
                        m = 2 * b + tt
                        js = [j for j in range(5) if m - 4 + j >= 0]
                        for j in js:
                            slot = (m - 4 + j) % 8
                            bank = 3 + (j % 2)
                            ps = PS[bank]
                            pk = "ps%d" % bank
                            for h in range(4):
                                mm(ps[:, h * 128:(h + 1) * 128], pk, kz[:, h // 2, h % 2, slot, :],
                                   qT[:, h // 2, tt * 128:(tt + 1) * 128], ["kz", "qT"])
                            S.add("dve", (lambda ps, j: lambda e: e.scalar_tensor_tensor(
                                out=s_sb[:], in0=ps[:].rearrange("p (h q) -> p h q", h=4), scalar=0.125,
                                in1=biasT[:, j, :, :], op0=ALU.mult, op1=ALU.add))(ps, j), [pk, "biasT"], ["s_sb"])
                            S.add("act", (lambda j: lambda e: e.activation(out=pT[:, j, :, :], in_=s_sb[:], func=AF.Exp))(j),
                                  ["s_sb"], [("pT", j)])
                            yield
                        for (pst2, pk2, which) in [(PS[5], "ps5", 0), (PS[6], "ps6", 1)]:
                            for h in range(4):
                                for j in js:
                                    slot = (m - 4 + j) % 8
                                    lhs = vring[:, slot, h * 64:(h + 1) * 64] if which == 0 else ones64[:]
                                    mm(pst2[64 * (h % 2):64 * (h % 2) + 64, (h // 2) * 128:(h // 2) * 128 + 128], pk2, lhs,
                                       pT[:, j, h, :], ["vring", "ones64", ("pT", j)], start=(j == js[0]), stop=(j == js[-1]))
                                yield
                        S.add("dve", lambda e: e.reciprocal(out=rden[:], in_=PS[6][:, 0:256].rearrange("p (a q) -> p a q", a=2)),
                              ["ps6"], ["rden"])
                        S.add("dve", (lambda tt: lambda e: e.tensor_tensor(
                            out=mixT[:, 0:2, tt * 128:(tt + 1) * 128],
                            in0=PS[5][:, 0:256].rearrange("p (a q) -> p a q", a=2), in1=rden[:], op=ALU.mult))(tt),
                            ["ps5", "rden"], ["mixT"])

                Z = [sbg(es, "Z%d" % i, [128, 2, NCK, 2, 64], BF16) for i in range(7)]
                Zk = ["Z%d" % i for i in range(7)]
                for i in range(7):
                    S.add("pool", (lambda i: lambda e: e.memset(Z[i][:], 0.0))(i), [], [Zk[i]])
                STh = sbg(es, "STh", [128, 2, 64], F32)
                STbdh = sbg(es, "STbdh", [128, 2, 128], BF16)
                Qtb = sbg(es, "Qtb", [128, 2, TB], BF16)
                Qab = sbg(es, "Qab", [128, 2, TB], BF16)
                S.add("pool", lambda e: e.memset(STh[:], 0.0), [], [("STh", 0), ("STh", 1)])
                S.add("pool", lambda e: e.memset(STbdh[:], 0.0), [], [("STbdh", 0), ("STbdh", 1)])
                dl = sbg(es, "dl", [128, 2 * NCK], F32)
                Mt = [sbg(es, "Mt%d" % i, [128, 128], F32 if i in (0, 1, 2, 3, 4, 12) else BF16) for i in range(15)]
                Mk = ["Mt%d" % i for i in range(15)]
                LANE_IDX = (0, 1, 2, 3, 4, 5, 6, 7, 8, 9, 10, 11)
                LMt = [Mt]
                LMk = [Mk]
                for ln in range(1, NCK):
                    tl = list(Mt)
                    kl = list(Mk)
                    for i in LANE_IDX:
                        tl[i] = sbg(es, "Mt%d_l%d" % (i, ln), [128, 128], F32 if i in (0, 1, 2, 3, 4) else BF16)
                        kl[i] = "Mt%d_l%d" % (i, ln)
                    LMt.append(tl)
                    LMk.append(kl)

                def run_lanes(gens):
                    active = list(gens)
                    while active:
                        for g_ in list(active):
                            try:
                                next(g_)
                            except StopIteration:
                                active.remove(g_)

                def half_copy(eng, dstZ, dk, srcT, sk, fn=None):
                    for h2 in range(2):
                        rows = slice(64 * h2, 64 * h2 + 64)
                        S.add(eng, (lambda rows, h2: lambda e: e.tensor_copy(
                            out=dstZ[rows, :, :, h2, :], in_=srcT[rows, :, :].rearrange("p a (c t) -> p a c t", c=NCK)))(rows, h2),
                            [sk], [dk])

                def half_mul(dstZ, dk, aT, ak, bT, bk):
                    for h2 in range(2):
                        rows = slice(64 * h2, 64 * h2 + 64)
                        S.add("dve", (lambda rows, h2: lambda e: e.tensor_tensor(
                            out=dstZ[rows, :, :, h2, :], in0=aT[rows, :, :].rearrange("p a (c t) -> p a c t", c=NCK),
                            in1=bT[rows, :, :].rearrange("p a (c t) -> p a c t", c=NCK), op=ALU.mult))(rows, h2),
                            [ak, bk], [dk])

                def cview(t):
                    return t[:].rearrange("p a (c t) -> p (a c) t", c=NCK)

                def mixer_hgrn(b):
                    q, f, iv, g, cum, key, t6, Qt, Qa, t9 = G[0], G[1], G[2], G[3], G[4], G[5], G[6], G[7], G[8], G[9]
                    Khz, Vz, Ktz = Z[0], Z[1], Z[2]
                    for pr in range(2):
                        inproj(6 + pr, q[:, pr, :], Gk[0])
                        inproj(8 + pr, f[:, pr, :], Gk[1])
                        inproj(10 + pr, iv[:, pr, :], Gk[2])
                        inproj(12 + pr, g[:, pr, :], Gk[3])
                    if "R" in MIXERS:
                        for i in range(7):
                            inproj(14 + i, pcx[:, i, 1:1 + TB], "pcx")
                    S.add("act", lambda e: e.activation(out=f[:], in_=f[:], func=AF.Sigmoid), [Gk[1]], [Gk[1]])
                    for pr in range(2):
                        S.add("dve", (lambda pr: lambda e: e.tensor_scalar(
                            out=f[:, pr, :], in0=f[:, pr, :], scalar1=omlb[:, l, pr:pr + 1], scalar2=lbv[:, l, pr:pr + 1],
                            op0=ALU.mult, op1=ALU.add))(pr), [Gk[1], "omlb", "lbv"], [Gk[1]])
                    S.add("dve", lambda e: e.tensor_scalar(out=key[:], in0=f[:], scalar1=-1.0, scalar2=1.0, op0=ALU.mult,
                                                           op1=ALU.add), [Gk[1]], [Gk[5]])
                    S.add("act", lambda e: e.activation(out=f[:], in_=f[:], func=AF.Ln), [Gk[1]], [Gk[1]])
                    for pr in range(2):
                        S.add("dve", (lambda pr: lambda e: e.tensor_tensor_scan(
                            out=cum[:, pr, :], data0=C("cm"), data1=f[:, pr, :], initial=0.0, op0=ALU.mult, op1=ALU.add))(pr),
                            [Gk[1], "consts"], [Gk[4]])
                    S.add("act", lambda e: e.activation(out=q[:], in_=q[:], func=AF.Silu), [Gk[0]], [Gk[0]])
                    cv_ = cview(cum)
                    S.add("dve", lambda e: e.tensor_tensor(out=cview(t6), in0=cv_, in1=cv_[:, :, 31:32].to_broadcast([128, 2 * NCK, 64]),
                                                           op=ALU.subtract), [Gk[4]], [Gk[6]])
                    S.add("act", lambda e: e.activation(out=t9[:], in_=t6[:], func=AF.Exp), [Gk[6]], [Gk[9]])
                    S.add("dve", lambda e: e.tensor_tensor(out=Qtb[:], in0=q[:], in1=t9[:], op=ALU.mult), [Gk[0], Gk[9]], ["Qtb"])
                    S.add("act", lambda e: e.activation(out=t9[:], in_=t6[:], func=AF.Exp, scale=-1.0), [Gk[6]], [Gk[9]])
                    half_mul(Khz, Zk[0], key, Gk[5], t9, Gk[9])
                    S.add("act", lambda e: e.activation(out=t9[:], in_=cum[:], func=AF.Exp), [Gk[4]], [Gk[9]])
                    S.add("dve", lambda e: e.tensor_tensor(out=Qab[:], in0=q[:], in1=t9[:], op=ALU.mult), [Gk[0], Gk[9]], ["Qab"])
                    S.add("dve", lambda e: e.tensor_tensor(out=cview(t6), in0=cv_, in1=cv_[:, :, 63:64].to_broadcast([128, 2 * NCK, 64]),
                                                           op=ALU.subtract), [Gk[4]], [Gk[6]])
                    S.add("act", lambda e: e.activation(out=t9[:], in_=t6[:], func=AF.Exp, scale=-1.0), [Gk[6]], [Gk[9]])
                    half_mul(Ktz, Zk[2], key, Gk[5], t9, Gk[9])
                    S.add("act", lambda e: e.activation(out=dl[:], in_=cv_[:, :, 63], func=AF.Exp), [Gk[4]], ["dl"])
                    half_copy("pool", Vz, Zk[1], iv, Gk[2])
                    S.add("act", lambda e: e.activation(out=g[:], in_=g[:], func=AF.Silu), [Gk[3]], [Gk[3]])
                    def hlane(pr, c, li):
                        ATs, Vbd, Vst, Ktbd = LMt[li][5], LMt[li][8], LMt[li][9], LMt[li][10]
                        AK, VbK, VsK, KtK = LMk[li][5], LMk[li][8], LMk[li][9], LMk[li][10]
                        cs = slice(c * 64, c * 64 + 64)
                        zK = Khz[:, pr, c, :, :].rearrange("p a t -> p (a t)")
                        zV = Vz[:, pr, c, :, :].rearrange("p a t -> p (a t)")
                        zT = Ktz[:, pr, c, :, :].rearrange("p a t -> p (a t)")
                        ps, pk = nb()
                        mm(ps[:, 0:64], pk, zK, Qtb[:, pr, cs], [Zk[0], "Qtb"])
                        S.add("dve", (lambda ps: lambda e: e.tensor_tensor(out=ATs[:, 0:64], in0=ps[:, 0:64], in1=C("iu"), op=ALU.mult))(ps),
                              [pk, "consts"], [AK])
                        yield
                        ps, pk = nb()
                        mm(ps[:, 0:128], pk, zV, Cb("I"), [Zk[1], "constsb"])
                        S.add("act", (lambda ps: lambda e: e.copy(out=Vbd[:], in_=ps[:, 0:128]))(ps), [pk], [VbK])
                        yield
                        ps, pk = nb()
                        mm(ps[:, 0:64], pk, zV, Cb("E2"), [Zk[1], "constsb"])
                        S.add("act", (lambda ps: lambda e: e.copy(out=Vst[:, 0:64], in_=ps[:, 0:64]))(ps), [pk], [VsK])
                        yield
                        ps, pk = nb()
                        mm(ps[:, 0:128], pk, zT, Cb("I"), [Zk[2], "constsb"])
                        S.add("dve", (lambda ps: lambda e: e.tensor_copy(out=Ktbd[:], in_=ps[:, 0:128]))(ps), [pk], [KtK])
                        yield

                    PSO = [(PS[2], "ps2"), (PS[7], "ps7")]

                    def hseq(pr, ccs):
                        psO, psOk = PSO[pr]
                        for c in ccs:
                            li = pr * 2 + c % 2
                            ATs, Vbd, Vst, Ktbd = LMt[li][5], LMt[li][8], LMt[li][9], LMt[li][10]
                            AK, VbK, VsK, KtK = LMk[li][5], LMk[li][8], LMk[li][9], LMk[li][10]
                            cs = slice(c * 64, c * 64 + 64)
                            mm(psO[:, cs], psOk, Vbd[:], ATs[:, 0:64], [VbK, AK], start=True, stop=False)
                            mm(psO[:, cs], psOk, STbdh[:, pr, :], Qab[:, pr, cs], [("STbdh", pr), "Qab"], start=False, stop=True)
                            yield
                            ps, pk = nb()
                            mm(ps[:, 0:64], pk, Ktbd[:], Vst[:, 0:64], [KtK, VsK])
                            S.add("dve", (lambda ps, pr, c: lambda e: e.scalar_tensor_tensor(
                                out=STh[:, pr, :], in0=STh[:, pr, :], scalar=dl[:, pr * NCK + c:pr * NCK + c + 1],
                                in1=ps[:, 0:64], op0=ALU.mult, op1=ALU.add))(ps, pr, c), [("STh", pr), "dl", pk], [("STh", pr)])
                            yield
                            for h2 in range(2):
                                rows = slice(64 * h2, 64 * h2 + 64)
                                S.add("act", (lambda rows, h2, pr: lambda e: e.copy(
                                    out=STbdh[rows, pr, 64 * h2:64 * h2 + 64], in_=STh[rows, pr, :]))(rows, h2, pr),
                                    [("STh", pr)], [("STbdh", pr)])
                            yield

                    for hf in range(NCK // 2):
                        ccs = (2 * hf, 2 * hf + 1)
                        run_lanes([hlane(pr, c, pr * 2 + c % 2) for pr in range(2) for c in ccs])
                        run_lanes([hseq(0, ccs), hseq(1, ccs)]) if os.environ.get('KZIP', '1') == '1' else (run_lanes([hseq(0, ccs)]), run_lanes([hseq(1, ccs)]))
                    def hpost(pr, psO, psOk):
                        S.add("act", (lambda pr: lambda e: e.activation(out=t6[:, 0, :], in_=psO[:, 0:TB], func=AF.Square))(pr),
                              [psOk], [Gk[6]])
                        mm(PS[3][:, 0:TB], "ps3", C("bd64"), t6[:, 0, :], ["consts", Gk[6]])
                        S.add("act", lambda e: e.activation(out=t6[:, 1, :], in_=PS[3][:, 0:TB], func=AF.Sqrt, bias=epsc[:, 0:1]),
                              ["ps3", "epsc"], [Gk[6]])
                        S.add("dve", lambda e: e.reciprocal(out=t6[:, 1, :], in_=t6[:, 1, :]), [Gk[6]], [Gk[6]])
                        S.add("dve", lambda e: e.tensor_tensor(out=t6[:, 0, :], in0=psO[:, 0:TB], in1=t6[:, 1, :], op=ALU.mult),
                              [psOk, Gk[6]], [Gk[6]])
                        S.add("dve", (lambda pr: lambda e: e.scalar_tensor_tensor(
                            out=mixT[:, 2 + pr, :], in0=g[:, pr, :], scalar=pv(l, "hnorm", pr), in1=t6[:, 0, :],
                            op0=ALU.mult, op1=ALU.mult))(pr), [Gk[3], "pvec", Gk[6]], ["mixT"])

                    for pr in range(2):
                        hpost(pr, PSO[pr][0], PSO[pr][1])

                pcx = sbg(es, "pcx", [128, 7, 1 + TB], F32)
                STr = sbg(es, "STr", [128, 2, 64], F32)
                STbdr = sbg(es, "STbdr", [128, 2, 128], BF16)
                STrb = sbg(es, "STrb", [128, 2, 64], BF16)
                S.add("pool", lambda e: e.memset(STrb[:], 0.0), [], [("STrb", 0), ("STrb", 1)])
                S.add("pool", lambda e: e.memset(pcx[:], 0.0), [], ["pcx"])
                S.add("pool", lambda e: e.memset(STr[:], 0.0), [], [("STr", 0), ("STr", 1)])
                S.add("pool", lambda e: e.memset(STbdr[:], 0.0), [], [("STbdr", 0), ("STbdr", 1)])
                S.add("pool", lambda e: e.memset(Mt[14][:], 0.0), [], [Mk[14]])
                WUU = [(Mt[12], Mt[13], Mt[14], Mk[12], Mk[13], Mk[14])]
                _w1 = sbg(es, "Wsb1", [128, 128], F32)
                _u1 = sbg(es, "Ust1", [128, 128], BF16)
                _b1 = sbg(es, "Ubd1", [128, 128], BF16)
                S.add("pool", lambda e: e.memset(_b1[:], 0.0), [], ["Ubd1"])
                WUU.append((_w1, _u1, _b1, "Wsb1", "Ust1", "Ubd1"))
                rb = [0]
                RBANKS = [3, 4, 5, 6]

                def nb():
                    i = RBANKS[rb[0] % len(RBANKS)]
                    rb[0] += 1
                    return PS[i], "ps%d" % i

                def mixer_rwkv(b):
                    r, k, v, lo, lw, a, g, kkn, bb, c, t10, t11, Rt, Ra = G
                    (rK, kK, vK, loK, lwK, aK, gK, kknK, bbK, cK, t10K, t11K, RtK, RaK) = Gk
                    Khz, Vz, Ktz, Bhz, Qtz, Qaz, Btz = Z
                    if "H" not in MIXERS:
                        for i in range(7):
                            inproj(14 + i, pcx[:, i, 1:1 + TB], "pcx")
                    dsts = [(r, rK, 0), (r, rK, 1), (k, kK, 0), (k, kK, 1), (v, vK, 0), (v, vK, 1), (lo, loK, 0)]
                    for i, (dt_, dk_, j) in enumerate(dsts):
                        S.add("dve", (lambda i: lambda e: e.tensor_tensor(out=t10[:, 0, :], in0=pcx[:, i, 0:TB],
                                                                          in1=pcx[:, i, 1:1 + TB], op=ALU.subtract))(i),
                              ["pcx"], [t10K])
                        S.add("dve", (lambda i, dt_, j: lambda e: e.scalar_tensor_tensor(
                            out=dt_[:, j, :], in0=t10[:, 0, :], scalar=pv(l, "mu", i), in1=pcx[:, i, 1:1 + TB],
                            op0=ALU.mult, op1=ALU.add))(i, dt_, j), [t10K, "pcx", "pvec"], [dk_])
                    S.add("pool", lambda e: e.tensor_copy(out=pcx[:, :, 0:1], in_=pcx[:, :, TB:TB + 1]), ["pcx"], ["pcx"])
                    if KR == 1:
                        return
                    S.add("act", lambda e: e.activation(out=lo[0:32, 1, :], in_=lo[0:32, 0, :], func=AF.Tanh), [loK], [loK])
                    S.add("dve", lambda e: e.tensor_copy(out=lo[32:64, 1, :], in_=lo[32:64, 0, :]), [loK], [loK])
                    S.add("act", lambda e: e.activation(out=lo[64:128, 1, :], in_=lo[64:128, 0, :], func=AF.Sigmoid), [loK], [loK])
                    for pr in range(2):
                        ps, pk = nb()
                        mm(ps[:, 0:TB], pk, lr3[:, 0, pr * 128:(pr + 1) * 128], lo[:, 1, :], ["lr3", loK])
                        S.add("act", (lambda ps, pr: lambda e: e.activation(out=lw[:, pr, :], in_=ps[:, 0:TB], func=AF.Sigmoid,
                                                                            bias=pv(l, "w0", pr)))(ps, pr), [pk, "pvec"], [lwK])
                        ps, pk = nb()
                        mm(ps[:, 0:TB], pk, lr3[:, 1, pr * 128:(pr + 1) * 128], lo[:, 1, :], ["lr3", loK])
                        S.add("act", (lambda ps, pr: lambda e: e.activation(out=a[:, pr, :], in_=ps[:, 0:TB], func=AF.Sigmoid,
                                                                            bias=pv(l, "a0", pr)))(ps, pr), [pk, "pvec"], [aK])
                        ps, pk = nb()
                        mm(ps[:, 0:TB], pk, lr3[:, 2, pr * 128:(pr + 1) * 128], lo[:, 1, :], ["lr3", loK])
                        S.add("act", (lambda ps, pr: lambda e: e.copy(out=g[:, pr, :], in_=ps[:, 0:TB]))(ps, pr), [pk], [gK])
                    S.add("dve", lambda e: e.tensor_scalar(out=lw[:], in0=lw[:], scalar1=-0.6065306597126334, scalar2=None,
                                                           op0=ALU.mult), [lwK], [lwK])
                    if KR == 2:
                        return
                    for pr in range(2):
                        S.add("dve", (lambda pr: lambda e: e.tensor_scalar(out=kkn[:, pr, :], in0=k[:, pr, :],
                                                                           scalar1=pv(l, "kk", pr), scalar2=None,
                                                                           op0=ALU.mult))(pr), [kK, "pvec"], [kknK])
                    S.add("act", lambda e: e.activation(out=t10[:], in_=kkn[:], func=AF.Square), [kknK], [t10K])
                    for pr in range(2):
                        ps, pk = nb()
                        mm(ps[:, 0:TB], pk, C("bd1"), t10[:, pr, :], ["consts", t10K])
                        S.add("act", (lambda ps, pr: lambda e: e.activation(out=t11[:, pr, :], in_=ps[:, 0:TB], func=AF.Sqrt,
                                                                            bias=epsc[:, 2:3]))(ps, pr), [pk, "epsc"], [t11K])
                    S.add("dve", lambda e: e.reciprocal(out=t11[:], in_=t11[:]), [t11K], [t11K])
                    S.add("dve", lambda e: e.tensor_tensor(out=kkn[:], in0=kkn[:], in1=t11[:], op=ALU.mult), [kknK, t11K], [kknK])
                    for pr in range(2):
                        S.add("dve", (lambda pr: lambda e: e.tensor_scalar(out=t10[:, pr, :], in0=a[:, pr, :], scalar1=epsc[:, 3:4],
                                                                           scalar2=pv(l, "ka", pr), op0=ALU.add,
                                                                           op1=ALU.mult))(pr), [aK, "pvec", "epsc"], [t10K])
                    S.add("dve", lambda e: e.scalar_tensor_tensor(out=k[:], in0=t10[:], scalar=1.0, in1=k[:], op0=ALU.add,
                                                                  op1=ALU.mult), [t10K, kK], [kK])
                    S.add("dve", lambda e: e.tensor_tensor(out=bb[:], in0=kkn[:], in1=a[:], op=ALU.mult), [kknK, aK], [bbK])
                    for pr in range(2):
                        S.add("dve", (lambda pr: lambda e: e.tensor_tensor_scan(
                            out=c[:, pr, :], data0=C("cm"), data1=lw[:, pr, :], initial=0.0, op0=ALU.mult, op1=ALU.add))(pr),
                            [lwK, "consts"], [cK])
                    cv_ = cview(c)
                    S.add("act", lambda e: e.activation(out=dl[:], in_=cv_[:, :, 63], func=AF.Exp), [cK], ["dl"])
                    S.add("dve", lambda e: e.tensor_tensor(out=cview(t10), in0=cv_, in1=cv_[:, :, 31:32].to_broadcast([128, 2 * NCK, 64]),
                                                           op=ALU.subtract), [cK], [t10K])
                    S.add("act", lambda e: e.activation(out=t11[:], in_=t10[:], func=AF.Exp), [t10K], [t11K])
                    S.add("dve", lambda e: e.tensor_tensor(out=Qtb[:], in0=r[:], in1=t11[:], op=ALU.mult), [rK, t11K], ["Qtb"])
                    S.add("act", lambda e: e.activation(out=t11[:], in_=t10[:], func=AF.Exp, scale=-1.0), [t10K], [t11K])
                    half_mul(Khz, Zk[0], k, kK, t11, t11K)
                    half_mul(Bhz, Zk[3], bb, bbK, t11, t11K)
                    S.add("dve", lambda e: e.tensor_tensor(out=t10[:], in0=t10[:], in1=lw[:], op=ALU.subtract), [t10K, lwK], [t10K])
                    S.add("act", lambda e: e.activation(out=t11[:], in_=t10[:], func=AF.Exp), [t10K], [t11K])
                    half_mul(Qtz, Zk[4], kkn, kknK, t11, t11K)
                    S.add("act", lambda e: e.activation(out=t11[:], in_=c[:], func=AF.Exp), [cK], [t11K])
                    S.add("dve", lambda e: e.tensor_tensor(out=Qab[:], in0=r[:], in1=t11[:], op=ALU.mult), [rK, t11K], ["Qab"])
                    S.add("dve", lambda e: e.tensor_tensor(out=t10[:], in0=c[:], in1=lw[:], op=ALU.subtract), [cK, lwK], [t10K])
                    S.add("act", lambda e: e.activation(out=t11[:], in_=t10[:], func=AF.Exp), [t10K], [t11K])
                    half_mul(Qaz, Zk[5], kkn, kknK, t11, t11K)
                    S.add("dve", lambda e: e.tensor_tensor(out=cview(t10), in0=cv_, in1=cv_[:, :, 63:64].to_broadcast([128, 2 * NCK, 64]),
                                                           op=ALU.subtract), [cK], [t10K])
                    S.add("act", lambda e: e.activation(out=t11[:], in_=t10[:], func=AF.Exp, scale=-1.0), [t10K], [t11K])
                    half_mul(Ktz, Zk[2], k, kK, t11, t11K)
                    half_mul(Btz, Zk[6], bb, bbK, t11, t11K)
                    half_copy("pool", Vz, Zk[1], v, vK)
                    if KR == 3:
                        return
                    Wsb, Ust, Ubd = Mt[12], Mt[13], Mt[14]
                    WsbK, UstK, UbdK = Mk[12], Mk[13], Mk[14]

                    def z2(zt, pr, cc):
                        return zt[:, pr, cc, :, :].rearrange("p a t -> p (a t)")

                    def rlane(pr, cc, li):
                        (X0, XT0, X1, XT1, Am, MkkT, Ark, Arb, Vbd, Vst, Ktbd, nBtbd) = LMt[li][0:12]
                        (X0K, XT0K, X1K, XT1K, AmK, MkkTK, ArkK, ArbK, VbdK, VstK, KtbdK, nBtbdK) = LMk[li][0:12]
                        cs = slice(cc * 64, cc * 64 + 64)
                        ps, pk = nb()
                        mm(ps[:, 0:128], pk, z2(Bhz, pr, cc), z2(Qtz, pr, cc), [Zk[3], Zk[4]])
                        S.add("dve", (lambda ps: lambda e: e.tensor_tensor(out=X0[:], in0=ps[:, 0:128], in1=C("nsu"), op=ALU.mult))(ps),
                              [pk, "consts"], [X0K])
                        yield
                        ps, pk = nb()
                        mm(ps[:, 0:128], pk, z2(Qtz, pr, cc), z2(Bhz, pr, cc), [Zk[3], Zk[4]])
                        S.add("dve", (lambda ps: lambda e: e.tensor_tensor(out=XT0[:], in0=ps[:, 0:128], in1=C("nsl"), op=ALU.mult))(ps),
                              [pk, "consts"], [XT0K])
                        yield
                        ps, pk = nb()
                        mm(ps[:, 0:128], pk, z2(Khz, pr, cc), z2(Qtz, pr, cc), [Zk[0], Zk[4]])
                        S.add("dve", (lambda ps: lambda e: e.tensor_tensor(out=MkkT[:], in0=ps[:, 0:128], in1=C("su"), op=ALU.mult))(ps),
                              [pk, "consts"], [MkkTK])
                        yield
                        ps, pk = nb()
                        mm(ps[:, 0:64], pk, z2(Khz, pr, cc), Qtb[:, pr, cs], [Zk[0], "Qtb"])
                        S.add("dve", (lambda ps: lambda e: e.tensor_tensor(out=Ark[:, 0:64], in0=ps[:, 0:64], in1=C("iu"), op=ALU.mult))(ps),
                              [pk, "consts"], [ArkK])
                        yield
                        ps, pk = nb()
                        mm(ps[:, 0:64], pk, z2(Bhz, pr, cc), Qtb[:, pr, cs], [Zk[3], "Qtb"])
                        S.add("dve", (lambda ps: lambda e: e.tensor_tensor(out=Arb[:, 0:64], in0=ps[:, 0:64], in1=C("niu"), op=ALU.mult))(ps),
                              [pk, "consts"], [ArbK])
                        yield
                        ps, pk = nb()
                        mm(ps[:, 0:128], pk, z2(Vz, pr, cc), Cb("I"), [Zk[1], "constsb"])
                        S.add("act", (lambda ps: lambda e: e.copy(out=Vbd[:], in_=ps[:, 0:128]))(ps), [pk], [VbdK])
                        yield
                        ps, pk = nb()
                        mm(ps[:, 0:64], pk, z2(Vz, pr, cc), Cb("E2"), [Zk[1], "constsb"])
                        S.add("act", (lambda ps: lambda e: e.copy(out=Vst[:, 0:64], in_=ps[:, 0:64]))(ps), [pk], [VstK])
                        yield
                        ps, pk = nb()
                        mm(ps[:, 0:128], pk, z2(Ktz, pr, cc), Cb("I"), [Zk[2], "constsb"])
                        S.add("act", (lambda ps: lambda e: e.copy(out=Ktbd[:], in_=ps[:, 0:128]))(ps), [pk], [KtbdK])
                        yield
                        ps, pk = nb()
                        mm(ps[:, 0:128], pk, z2(Btz, pr, cc), Cb("I"), [Zk[6], "constsb"])
                        S.add("act", (lambda ps: lambda e: e.mul(out=nBtbd[:], in_=ps[:, 0:128], mul=-1.0))(ps), [pk], [nBtbdK])
                        yield
                        S.add("dve", lambda e: e.tensor_tensor(out=Am[:], in0=X0[:], in1=C("I"), op=ALU.add), [X0K, "consts"], [AmK])
                        P, PK, PT, PTK = X0, X0K, XT0, XT0K
                        Pn, PnK, PTn, PTnK = X1, X1K, XT1, XT1K
                        for j in range(1, 6):
                            ps, pk = nb()
                            mm(ps[:, 0:128], pk, P[:], PT[:], [PK, PTK])
                            S.add("act", (lambda ps, PTn: lambda e: e.copy(out=PTn[:], in_=ps[:, 0:128]))(ps, PTn), [pk], [PTnK])
                            yield
                            if j <= 4:
                                ps2, pk2 = nb()
                                mm(ps2[:, 0:128], pk2, PT[:], P[:], [PK, PTK])
                                S.add("dve", (lambda ps2, Pn: lambda e: e.tensor_copy(out=Pn[:], in_=ps2[:, 0:128]))(ps2, Pn),
                                      [pk2], [PnK])
                                yield
                            ps3, pk3 = nb()
                            mm(ps3[:, 0:128], pk3, PTn[:], Am[:], [PTnK, AmK])
                            S.add("dve", (lambda ps3: lambda e: e.tensor_tensor(out=Am[:], in0=Am[:], in1=ps3[:, 0:128], op=ALU.add))(ps3),
                                  [pk3, AmK], [AmK])
                            yield
                            P, PK, PT, PTK, Pn, PnK, PTn, PTnK = Pn, PnK, PTn, PTnK, P, PK, PT, PTK

                    PSY = [(PS[2], "ps2"), (PS[7], "ps7")]

                    def rseq(pr, ccs):
                        psY, psYk = PSY[pr]
                        Wsb, Ust, Ubd, WsbK, UstK, UbdK = WUU[pr]
                        for cc in ccs:
                            li = pr * 2 + cc % 2
                            (X0, XT0, X1, XT1, Am, MkkT, Ark, Arb, Vbd, Vst, Ktbd, nBtbd) = LMt[li][0:12]
                            (X0K, XT0K, X1K, XT1K, AmK, MkkTK, ArkK, ArbK, VbdK, VstK, KtbdK, nBtbdK) = LMk[li][0:12]
                            cs = slice(cc * 64, cc * 64 + 64)
                            ps, pk = nb()
                            mm(ps[:, 0:64], pk, MkkT[:], Vst[:, 0:64], [MkkTK, VstK], start=True, stop=False)
                            mm(ps[:, 0:64], pk, z2(Qaz, pr, cc), STrb[:, pr, :], [Zk[5], ("STrb", pr)], start=False, stop=True)
                            S.add("act", (lambda ps: lambda e: e.copy(out=Wsb[:, 0:64], in_=ps[:, 0:64]))(ps), [pk], [WsbK])
                            yield
                            ps, pk = nb()
                            mm(ps[:, 0:64], pk, Am[:], Wsb[:, 0:64], [AmK, WsbK])
                            S.add("act", (lambda ps: lambda e: e.copy(out=Ust[:, 0:64], in_=ps[:, 0:64]))(ps), [pk], [UstK])
                            yield
                            for h2 in range(2):
                                rows = slice(64 * h2, 64 * h2 + 64)
                                S.add("act", (lambda rows, h2: lambda e: e.copy(
                                    out=Ubd[rows, 64 * h2:64 * h2 + 64], in_=Ust[rows, 0:64]))(rows, h2), [UstK], [UbdK])
                            yield
                            mm(psY[:, cs], psYk, STbdr[:, pr, :], Qab[:, pr, cs], [("STbdr", pr), "Qab"], start=True, stop=False)
                            mm(psY[:, cs], psYk, Vbd[:], Ark[:, 0:64], [VbdK, ArkK], start=False, stop=False)
                            mm(psY[:, cs], psYk, Ubd[:], Arb[:, 0:64], [UbdK, ArbK], start=False, stop=True)
                            yield
                            ps, pk = nb()
                            mm(ps[:, 0:64], pk, Ktbd[:], Vst[:, 0:64], [KtbdK, VstK], start=True, stop=False)
                            mm(ps[:, 0:64], pk, nBtbd[:], Ust[:, 0:64], [nBtbdK, UstK], start=False, stop=True)
                            S.add("dve", (lambda ps, pr, cc: lambda e: e.scalar_tensor_tensor(
                                out=STr[:, pr, :], in0=STr[:, pr, :], scalar=dl[:, pr * NCK + cc:pr * NCK + cc + 1],
                                in1=ps[:, 0:64], op0=ALU.mult, op1=ALU.add))(ps, pr, cc), [("STr", pr), "dl", pk], [("STr", pr)])
                            yield
                            for h2 in range(2):
                                rows = slice(64 * h2, 64 * h2 + 64)
                                S.add("act", (lambda rows, h2, pr: lambda e: e.copy(
                                    out=STbdr[rows, pr, 64 * h2:64 * h2 + 64], in_=STr[rows, pr, :]))(rows, h2, pr),
                                    [("STr", pr)], [("STbdr", pr)])
                            S.add("act", (lambda pr: lambda e: e.copy(out=STrb[:, pr, :], in_=STr[:, pr, :]))(pr),
                                  [("STr", pr)], [("STrb", pr)])
                            yield

                    for hf in range(NCK // 2):
                        ccs = (2 * hf, 2 * hf + 1)
                        run_lanes([rlane(pr, cc, pr * 2 + cc % 2) for pr in range(2) for cc in ccs])
                        run_lanes([rseq(0, ccs), rseq(1, ccs)]) if os.environ.get('KZIP', '1') == '1' else (run_lanes([rseq(0, ccs)]), run_lanes([rseq(1, ccs)]))
                    def rpost(pr, psY, psYk):
                        ysb2 = t10[:, 0, :]
                        yc2 = t10[:, 1, :]
                        sq2 = t11[:, 0, :]
                        rs2 = t11[:, 1, :]
                        S.add("act", lambda e: e.copy(out=ysb2, in_=psY[:, 0:TB]), [psYk], [t10K])
                        ps, pk = nb()
                        mm(ps[:, 0:TB], pk, C("bd64"), ysb2, ["consts", t10K])
                        S.add("dve", (lambda ps: lambda e: e.tensor_tensor(out=yc2, in0=ysb2, in1=ps[:, 0:TB], op=ALU.subtract))(ps),
                              [pk, t10K], [t10K])
                        S.add("act", lambda e: e.activation(out=sq2, in_=yc2, func=AF.Square), [t10K], [t11K])
                        ps, pk = nb()
                        mm(ps[:, 0:TB], pk, C("bd64"), sq2, ["consts", t11K])
                        S.add("act", (lambda ps: lambda e: e.activation(out=rs2, in_=ps[:, 0:TB], func=AF.Sqrt, bias=epsc[:, 1:2]))(ps),
                              [pk, "epsc"], [t11K])
                        S.add("dve", lambda e: e.reciprocal(out=rs2, in_=rs2), [t11K], [t11K])
                        S.add("dve", lambda e: e.tensor_tensor(out=yc2, in0=yc2, in1=rs2, op=ALU.mult), [t10K, t11K], [t10K])
                        S.add("dve", (lambda pr: lambda e: e.tensor_scalar(out=yc2, in0=yc2, scalar1=pv(l, "lnw", pr),
                                                                           scalar2=pv(l, "lnb", pr), op0=ALU.mult,
                                                                           op1=ALU.add))(pr), [t10K, "pvec"], [t10K])
                        S.add("dve", (lambda pr: lambda e: e.scalar_tensor_tensor(out=sq2, in0=r[:, pr, :], scalar=pv(l, "rk", pr),
                                                                                  in1=k[:, pr, :], op0=ALU.mult,
                                                                                  op1=ALU.mult))(pr), [rK, kK, "pvec"], [t11K])
                        ps, pk = nb()
                        mm(ps[:, 0:TB], pk, C("bd1"), sq2, ["consts", t11K])
                        S.add("dve", (lambda ps, pr: lambda e: e.tensor_tensor(out=sq2, in0=ps[:, 0:TB], in1=v[:, pr, :], op=ALU.mult))(ps, pr),
                              [pk, vK], [t11K])
                        S.add("dve", lambda e: e.tensor_tensor(out=yc2, in0=yc2, in1=sq2, op=ALU.add), [t10K, t11K], [t10K])
                        S.add("dve", (lambda pr: lambda e: e.tensor_tensor(out=mixT[:, 4 + pr, :], in0=yc2, in1=g[:, pr, :],
                                                                           op=ALU.mult))(pr), [t10K, gK], ["mixT"])

                    for pr in range(2):
                        rpost(pr, PSY[pr][0], PSY[pr][1])

                for b in range(NB):
                    S.dma("sp", xb[:], dview(src, b), reads=[(srck, b)], writes=["xbA"])
                    rmsnorm(xb, "xbA", hT, "hTA", rstd, "rstdA", l, "nmixpre", 2)
                    if "H" in MIXERS:
                        mixer_hgrn(b)
                    if "R" in MIXERS:
                        mixer_rwkv(b)
                    run_lanes(([mixer_attn(b)] if "A" in MIXERS else []) + ([mixer_lru(b)] if "L" in MIXERS else []))
                    if debug_mix and l == int(os.environ.get("KDBGL", "0")):
                        S.add("act", lambda e: e.copy(out=yT[:], in_=mixT[:]), ["mixT"], ["yTA"])
                        S.dma("sp", dview(mix_dbg, b), yT[:], reads=["yTA"], writes=[("mixdbg", b)])
                    for oc in range(8):
                        ps, pk = half(dcnt[0] % 2)
                        dcnt[0] += 1
                        for kc in range(8):
                            mm(ps, pk, wout[:, kc, oc * 128:(oc + 1) * 128], mixT[:, kc, :], [("wout", kc), "mixT"],
                               start=(kc == 0), stop=(kc == 7))
                        S.add("act", (lambda ps, oc: lambda e: e.copy(out=yT[:, oc, :], in_=ps))(ps, oc), [pk], ["yTA"])
                    postnorm_residual(yT, "yTA", xb, "xbA", hT, "hTA", rstd, "rstdA", l, "nmixpost", 2)
                    S.dma("sp", dview(dst, b), yT[:], reads=["yTA"], writes=[(dstk, b)])
                S.flush()

        KSTOP = int(os.environ.get("KSTOP", "0"))
        cur, curk = xT_in, "xin"
        for l in range(L if KSTOP != 1 else 0):
            if do_mix:
                phase_mix(l, cur, curk, scr[0], "scr0_%d" % l)
                cur, curk = scr[0], "scr0_%d" % l
            if do_mlp:
                last = (l == L - 1)
                dst, dstk = (yT_out, "yout") if last else (scr[1], "scr1_%d" % l)
                phase_mlp(l, cur, curk, dst, dstk)
                cur, curk = dst, dstk
        S.finish("sp")
        print("sched stats: ops", S.nops, "waits", S.nwait, "cnt", S.cnt, "ndma", len(S.dma_sigs), flush=True)
    return nc


def _feat(v):
    v = np.asarray(v, np.float32)
    return np.ascontiguousarray(v.reshape(-1, 128).T)


def pack_params(inp, L):
    pvec = np.zeros((128, L, NPV), np.float32)
    names = dict(nmixpre="norm_mix_pre", nmixpost="norm_mix_post", nmlppre="norm_mlp_pre", nmlppost="norm_mlp_post",
                 lblog="hgrn_lb_logits", hnorm="hgrn_norm", mu="rwkv_mu", w0="rwkv_w0", a0="rwkv_a0", kk="rwkv_k_k",
                 ka="rwkv_k_a", lnw="rwkv_ln_w", lnb="rwkv_ln_b", cb="lru_conv_b", ba="lru_ba", bx="lru_bx",
                 lam="lru_lambda")
    for l in range(L):
        for k, src in names.items():
            o, n = PVL[k]
            pvec[:, l, o:o + n] = _feat(inp[src][l])
        o, n = PVL["rk"]
        pvec[:, l, o:o + n] = _feat(np.asarray(inp["rwkv_r_k"][l]).reshape(-1))
        for j in range(4):
            o, n = PVL["cw%d" % j]
            pvec[:, l, o:o + n] = _feat(inp["lru_conv_w"][l][j])
    lr3 = np.zeros((128, L, 3, 256), np.float32)
    lruw = np.zeros((128, L, 2, 2, 128), np.float32)
    for l in range(L):
        lr3[0:32, l, 0, :] = inp["rwkv_w2"][l]
        lr3[32:64, l, 1, :] = inp["rwkv_a2"][l]
        lr3[64:128, l, 2, :] = inp["rwkv_g2"][l]
        for i, nm in enumerate(["lru_wa", "lru_wx"]):
            w = np.asarray(inp[nm][l])
            for n in range(4):
                ch, hb = n // 2, n % 2
                lruw[hb * 64:(hb + 1) * 64, l, i, ch, hb * 64:(hb + 1) * 64] = w[n]
    ki = np.arange(128)[:, None]
    qi = np.arange(128)[None, :]
    biasT = np.zeros((128, L, 5, 4, 128), np.float32)
    for j in range(5):
        rel = (-8 + 2 * j) * 64 + ki - qi
        kc = -8 + 2 * j + ki // 64
        cq = qi // 64
        ok = ((cq - kc) >= 0) & ((cq - kc) <= 8)
        idx = np.clip(rel, -256, 256) + 256
        for l in range(L):
            tab = np.asarray(inp["attn_rel_bias"][l], np.float32)
            for h in range(4):
                biasT[:, l, j, h, :] = np.where(ok, tab[h][idx], np.float32(-30000.0))
    return pvec, lr3, lruw, biasT


def run(inputs, S_LEN, L, n_cores=8, **kw):
    x = np.asarray(inputs["x"], np.float32)
    B = x.shape[0]
    pvec, lr3, lruw, biasT = pack_params(inputs, L)
    consts = make_consts()
    nc = build(S_LEN, L, **kw)
    shared = dict(w_in=np.ascontiguousarray(inputs["w_in"][:L], dtype=np.float32),
                  w_out=np.ascontiguousarray(inputs["w_out"][:L], dtype=np.float32),
                  mlp_w1=np.ascontiguousarray(inputs["mlp_w1"][:L], dtype=np.float32),
                  mlp_w2=np.ascontiguousarray(inputs["mlp_w2"][:L], dtype=np.float32),
                  pvec=pvec, lr3=lr3, lruw=lruw, biasT=biasT, consts=consts)
    in_maps = []
    for c in range(n_cores):
        b = c % B
        m = dict(shared)
        m["xT"] = np.ascontiguousarray(x[b, :S_LEN].T)
        in_maps.append(m)
    res = run_bass_kernel_spmd(nc, in_maps, core_ids=list(range(n_cores)))
    return res


def kernel(**inputs):
    x = np.asarray(inputs["x"])
    B, S_LEN, _ = x.shape
    L = np.asarray(inputs["w_in"]).shape[0]
    res = run(inputs, S_LEN, L)
    out = np.stack([np.ascontiguousarray(res.results[b]["yT"].T) for b in range(B)], 0)
    return out.astype(np.float32)
```

```python
import os
import numpy as np
from contextlib import ExitStack
import concourse.bass as bass
import concourse.mybir as mybir
from concourse.bass_utils import run_bass_kernel_spmd

F32 = mybir.dt.float32
BF16 = mybir.dt.bfloat16
AF = mybir.ActivationFunctionType
ALU = mybir.AluOpType
AX = mybir.AxisListType

D = 1024
DIN = 3200
HID = 4096
TB = 256
CH = 64
NCK = TB // CH
ENGS = ("pe", "act", "dve", "pool", "sp")

PVL = {}


def _pv_layout():
    off = 0
    for name, n in [("nmixpre", 8), ("nmixpost", 8), ("nmlppre", 8), ("nmlppost", 8), ("lblog", 2), ("hnorm", 2),
                    ("mu", 7), ("w0", 2), ("a0", 2), ("kk", 2), ("ka", 2), ("rk", 2), ("lnw", 2), ("lnb", 2),
                    ("cw0", 2), ("cw1", 2), ("cw2", 2), ("cw3", 2), ("cb", 2), ("ba", 2), ("bx", 2), ("lam", 2)]:
        PVL[name] = (off, n)
        off += n
    return off


NPV = _pv_layout()

CL = {}


def _c_layout():
    off = 0
    for name, n in [("I", 128), ("E2", 64), ("su", 128), ("nsu", 128), ("nsl", 128), ("iu", 64), ("niu", 64),
                    ("cm", TB), ("bd1", 128), ("bd64", 128)]:
        CL[name] = (off, n)
        off += n
    return off


NCONST = _c_layout()


def make_consts():
    c = np.zeros((128, NCONST), np.float32)
    p = np.arange(128)

    def put(name, arr):
        o, n = CL[name]
        c[:, o:o + n] = arr

    put("I", np.eye(128, dtype=np.float32))
    put("E2", np.concatenate([np.eye(64), np.eye(64)], 0))
    blk = (p[:, None] // 64) == (p[None, :] // 64)
    s = p[:, None] % 64
    t = p[None, :] % 64
    su = (blk & (s < t)).astype(np.float32)
    put("su", su)
    put("nsu", -su)
    put("nsl", -(blk & (s > t)).astype(np.float32))
    iu = ((p[:, None] % 64) <= np.arange(64)[None, :]).astype(np.float32)
    put("iu", iu)
    put("niu", -iu)
    cm = np.ones((128, TB), np.float32)
    cm[:, ::CH] = 0.0
    put("cm", cm)
    put("bd1", blk.astype(np.float32))
    put("bd64", blk.astype(np.float32) / 64.0)
    return c


class Sched:
    NDMA = 24

    def __init__(self, nc, sems_c, sems_d, sems_sw, swscratch):
        self.nc = nc
        self.alias = {}
        self.sw = sems_sw
        self.swscratch = swscratch
        self.sc = sems_c
        self.sd = sems_d
        self.pending = []
        self.last_w = {}
        self.readers = {}
        self.cnt = {e: 0 for e in ENGS}
        self.known = {e: {} for e in ENGS}
        self.dma_sigs = []
        self.nops = 0
        self.nwait = 0

    def eng(self, name):
        nc = self.nc
        return {"pe": nc.tensor, "act": nc.scalar, "dve": nc.vector, "pool": nc.gpsimd, "sp": nc.sync}[name]

    def _exp(self, keys):
        out = []
        for k in keys:
            if k in self.alias:
                out.extend(self.alias[k])
            else:
                out.append(k)
        return tuple(out)

    def add(self, eng, fn, reads=(), writes=(), dma=False):
        reads = self._exp(reads)
        writes = self._exp(writes)
        self.pending.append(dict(eng=eng, fn=fn, reads=tuple(reads), writes=tuple(writes), dma=dma, sig=None,
                                 force=False))

    def dma(self, eng, out, in_, reads=(), writes=()):
        self.add(eng, lambda e: e.dma_start(out=out, in_=in_), reads, writes, dma=True)

    def flush(self):
        ops = self.pending
        self.pending = []
        for op in ops:
            d = []
            raw = []
            for k in op["reads"]:
                w = self.last_w.get(k)
                if w is not None:
                    d.append(w)
                    raw.append(w)
            for k in op["writes"]:
                w = self.last_w.get(k)
                if w is not None:
                    d.append(w)
                for r in self.readers.get(k, ()):
                    d.append(r)
            dd = []
            seen = set()
            for oj in d:
                if oj is op or id(oj) in seen:
                    continue
                seen.add(id(oj))
                if (not oj["dma"]) and oj["eng"] == op["eng"] and not op["dma"]:
                    if op["eng"] == "pe":
                        continue
                    pass
                dd.append(oj)
            op["deps"] = dd
            for oj in dd:
                oj["force"] = True
            for k in op["reads"]:
                self.readers.setdefault(k, []).append(op)
            for k in op["writes"]:
                self.last_w[k] = op
                self.readers[k] = []
        for k, w in self.last_w.items():
            w["force"] = True
        for k, rs in self.readers.items():
            for r in rs:
                r["force"] = True
        sw_run = []
        for idx, op in enumerate(ops):
            e = op["eng"]
            engine = self.eng(e)
            need = {}
            swdma = op["dma"] and e == "pool"
            if swdma:
                pass
            elif op["dma"]:
                di = len(self.dma_sigs)
                sk = ("dma", di % self.NDMA)
                v = 16 * (di // self.NDMA + 1)
                op["sig"] = (sk, v)
                if di >= self.NDMA:
                    psk, pv = self.dma_sigs[di - self.NDMA]
                    need[psk] = pv
                self.dma_sigs.append((sk, v))
            elif op["force"]:
                self.cnt[e] += 1
                op["sig"] = ((e,), self.cnt[e])
            for oj in op["deps"]:
                if oj["sig"] is None and swdma:
                    continue
                sk, v = oj["sig"]
                if need.get(sk, 0) < v:
                    need[sk] = v
            for sk, v in need.items():
                if self.known[e].get(sk, 0) >= v:
                    continue
                sem = self.sd[sk[1]] if sk[0] == "dma" else self.sc[sk[0]]
                engine.wait_ge(sem, v)
                self.known[e][sk] = v
                self.nwait += 1
            ins = op["fn"](engine)
            self.nops += 1
            if swdma:
                sem = self.sw[len(sw_run)]
                ins.then_inc(sem, 16)
                sw_run.append((op, sem))
                nxt = ops[idx + 1] if idx + 1 < len(ops) else None
                if len(sw_run) == len(self.sw) or nxt is None or not (nxt["dma"] and nxt["eng"] == "pool"):
                    for (o2, sem2) in sw_run:
                        engine.wait_ge(sem2, 16)
                        engine.sem_clear(sem2)
                    self.cnt["pool"] += 1
                    engine.memset(self.swscratch, 0.0).then_inc(self.sc["pool"], 1)
                    for (o2, sem2) in sw_run:
                        o2["sig"] = (("pool",), self.cnt["pool"])
                    sw_run = []
            elif op["sig"] is not None:
                sk, v = op["sig"]
                if sk[0] == "dma":
                    ins.then_inc(self.sd[sk[1]], 16)
                else:
                    ins.then_inc(self.sc[sk[0]], 1)
            op["fn"] = None
            op["deps"] = None

    def finish(self, eng="sp"):
        self.flush()
        engine = self.eng(eng)
        tot = {}
        for sk, v in self.dma_sigs:
            tot[sk] = max(tot.get(sk, 0), v)
        for sk, v in tot.items():
            engine.wait_ge(self.sd[sk[1]], v)


def build(S_LEN, L, debug_mix=False, do_mix=True, do_mlp=True):
    NB = S_LEN // TB
    nc = bass.Bass("TRN2", target_bir_lowering=False)

    def din(name, shape):
        return nc.dram_tensor(name, list(shape), F32, kind="ExternalInput").ap()

    xT_in = din("xT", [D, S_LEN])
    w_in = din("w_in", [L, D, DIN])
    w_out = din("w_out", [L, D, D])
    w1d = din("mlp_w1", [L, D, HID])
    w2d = din("mlp_w2", [L, HID, D])
    pvec_d = din("pvec", [128, L, NPV])
    lr3_d = din("lr3", [128, L, 3, 256])
    lruw_d = din("lruw", [128, L, 2, 2, 128])
    bias_d = din("biasT", [128, L, 5, 4, 128])
    const_d = din("consts", [128, NCONST])
    yT_out = nc.dram_tensor("yT", [D, S_LEN], F32, kind="ExternalOutput").ap()
    scr = [nc.dram_tensor("scr%d" % i, [D, S_LEN], F32, kind="Internal").ap() for i in range(2)]
    if debug_mix:
        mix_dbg = nc.dram_tensor("mixdbg", [D, S_LEN], F32, kind="ExternalOutput").ap()

    def dview(ap, b):
        return ap.rearrange("(k p) s -> p k s", p=128)[:, :, b * TB:(b + 1) * TB]

    with ExitStack() as es0:
        sems_c = {e: es0.enter_context(nc.semaphore("s_" + e)) for e in ENGS}
        sems_d = [es0.enter_context(nc.semaphore("d_%d" % i)) for i in range(Sched.NDMA)]
        sems_sw = [es0.enter_context(nc.semaphore("w_%d" % i)) for i in range(8)]
        swscr = es0.enter_context(nc.sbuf_tensor("swscr", [128, 2], F32))
        S = Sched(nc, sems_c, sems_d, sems_sw, swscr[:, 0:1])

        _uid = [0]

        def sbg(es, name, shape, dt):
            _uid[0] += 1
            return es.enter_context(nc.sbuf_tensor("%s_u%d" % (name, _uid[0]), list(shape), dt))

        PS = [es0.enter_context(nc.psum_tensor("psb%d" % i, [128, 512], F32)) for i in range(8)]

        def half(i):
            return PS[i % 8][:, 0:256], "ps%d" % (i % 8)

        def quarter(i):
            return PS[i // 4][:, (i % 4) * 128:(i % 4) * 128 + 128], "pq%d" % i

        consts = sbg(es0, "consts", [128, NCONST], F32)
        pvec = sbg(es0, "pvec", [128, L, NPV], F32)
        onesD = sbg(es0, "onesD", [128, 128], BF16)
        ones64 = sbg(es0, "ones64", [128, 64], BF16)
        lbv = sbg(es0, "lbv", [128, L, 2], F32)
        omlb = sbg(es0, "omlb", [128, L, 2], F32)
        nsp8 = sbg(es0, "nsp8", [128, L, 2], F32)
        tmpv = sbg(es0, "tmpv", [128, L, 2], F32)
        tmpv2 = sbg(es0, "tmpv2", [128, 2], F32)
        epsc = sbg(es0, "epsc", [128, 4], F32)

        constsb = sbg(es0, "constsb", [128, NCONST], BF16)

        def C(name):
            o, n = CL[name]
            return consts[:, o:o + n]

        def Cb(name):
            o, n = CL[name]
            return constsb[:, o:o + n]

        def pv(l, name, j=0):
            o, n = PVL[name]
            return pvec[:, l, o + j:o + j + 1]

        S.dma("sp", consts[:], const_d, writes=["consts"])
        S.dma("sp", pvec[:], pvec_d, writes=["pvec"])
        S.add("dve", lambda e: e.tensor_copy(out=constsb[:], in_=consts[:]), ["consts"], ["constsb"])
        S.add("pool", lambda e: e.memset(onesD[:], 1.0 / D), [], ["onesD"])
        S.add("pool", lambda e: e.memset(ones64[:], 1.0), [], ["ones64"])
        S.add("pool", lambda e: e.memset(epsc[:, 0:1], 1e-6), [], ["epsc"])
        S.add("pool", lambda e: e.memset(epsc[:, 1:2], 64e-5), [], ["epsc"])
        S.add("pool", lambda e: e.memset(epsc[:, 2:3], 1e-24), [], ["epsc"])
        S.add("pool", lambda e: e.memset(epsc[:, 3:4], -1.0), [], ["epsc"])
        o_lb = PVL["lblog"][0]
        o_lam = PVL["lam"][0]
        S.add("act", lambda e: e.activation(out=tmpv[:], in_=pvec[:, :, o_lb:o_lb + 2], func=AF.Exp), ["pvec"], ["tmpv"])
        S.add("dve", lambda e: e.tensor_reduce(out=tmpv2[:], in_=tmpv[:].rearrange("p l c -> p c l"), axis=AX.X,
                                                op=ALU.add), ["tmpv"], ["tmpv2"])
        S.add("dve", lambda e: e.reciprocal(out=tmpv2[:], in_=tmpv2[:]), ["tmpv2"], ["tmpv2"])
        S.add("dve", lambda e: e.tensor_tensor(out=tmpv[:], in0=tmpv[:], in1=tmpv2[:].unsqueeze(1).to_broadcast([128, L, 2]),
                                                op=ALU.mult), ["tmpv", "tmpv2"], ["tmpv"])
        S.add("dve", lambda e: e.memset(lbv[:, 0, :], 0.0), [], ["lbv"])
        for l in range(1, L):
            S.add("dve", (lambda l: lambda e: e.tensor_tensor(out=lbv[:, l, :], in0=lbv[:, l - 1, :], in1=tmpv[:, l, :],
                                                              op=ALU.add))(l), ["lbv", "tmpv"], ["lbv"])
        S.add("dve", lambda e: e.tensor_scalar(out=omlb[:], in0=lbv[:], scalar1=-1.0, scalar2=1.0, op0=ALU.mult,
                                               op1=ALU.add), ["lbv"], ["omlb"])
        S.add("act", lambda e: e.activation(out=nsp8[:], in_=pvec[:, :, o_lam:o_lam + 2], func=AF.Exp, scale=-1.0),
              ["pvec"], ["nsp8"])
        S.add("act", lambda e: e.activation(out=nsp8[:], in_=nsp8[:], func=AF.Ln, bias=1.0), ["nsp8"], ["nsp8"])
        S.add("dve", lambda e: e.tensor_scalar(out=nsp8[:], in0=nsp8[:], scalar1=-8.0, scalar2=None, op0=ALU.mult),
              ["nsp8"], ["nsp8"])
        S.flush()

        def mm(out, outk, lhsT, rhs, reads, start=True, stop=True):
            S.add("pe", lambda e: e.matmul(out, lhsT=lhsT, rhs=rhs, start=start, stop=stop), reads, [outk])

        def load_cast(dst, dstk, src, stg, stgk, n, i, view=None):
            sv = stg[:, 0:n]
            if view is not None:
                sv = view(sv)
            S.dma("sp", sv, src, writes=[stgk])
            eng = ("act", "dve", "pool")[i % 3]
            if eng == "act":
                S.add("act", lambda e: e.copy(out=dst, in_=sv), [stgk], [dstk])
            else:
                S.add(eng, lambda e: e.tensor_copy(out=dst, in_=sv), [stgk], [dstk])

        def rmsnorm(X, xk, hT, hk, rstd, rk, l, gname, slot):
            ps, pk = half(slot)
            S.add("act", lambda e: e.activation(out=hT[:], in_=X[:], func=AF.Square), [xk], [hk])
            for kc in range(8):
                mm(ps, pk, onesD[:], hT[:, kc, :], [hk, "onesD"], start=(kc == 0), stop=(kc == 7))
            S.add("act", lambda e: e.activation(out=rstd[:], in_=ps, func=AF.Sqrt, bias=epsc[:, 0:1]), [pk, "epsc"], [rk])
            S.add("dve", lambda e: e.reciprocal(out=rstd[:], in_=rstd[:]), [rk], [rk])
            for kc in range(8):
                S.add("dve", (lambda kc: lambda e: e.scalar_tensor_tensor(out=hT[:, kc, :], in0=X[:, kc, :],
                                                                          scalar=pv(l, gname, kc), in1=rstd[:],
                                                                          op0=ALU.mult, op1=ALU.mult))(kc),
                      [xk, rk, "pvec"], [hk])

        def postnorm_residual(Y, yk, X, xk, sq, sqk, rstd, rk, l, gname, slot):
            ps, pk = half(slot)
            S.add("act", lambda e: e.activation(out=sq[:], in_=Y[:], func=AF.Square), [yk], [sqk])
            for kc in range(8):
                mm(ps, pk, onesD[:], sq[:, kc, :], [sqk, "onesD"], start=(kc == 0), stop=(kc == 7))
            S.add("act", lambda e: e.activation(out=rstd[:], in_=ps, func=AF.Sqrt, bias=epsc[:, 0:1]), [pk, "epsc"], [rk])
            S.add("dve", lambda e: e.reciprocal(out=rstd[:], in_=rstd[:]), [rk], [rk])
            S.add("pool", lambda e: e.tensor_tensor(out=Y[:], in0=Y[:], in1=rstd[:].unsqueeze(1).to_broadcast([128, 8, TB]),
                                                    op=ALU.mult), [yk, rk], [yk])
            for kc in range(8):
                S.add("dve", (lambda kc: lambda e: e.scalar_tensor_tensor(out=Y[:, kc, :], in0=Y[:, kc, :],
                                                                          scalar=pv(l, gname, kc), in1=X[:, kc, :],
                                                                          op0=ALU.mult, op1=ALU.add))(kc),
                      [yk, xk, "pvec"], [yk])

        def phase_mlp(l, src, srck, dst, dstk):
            with ExitStack() as es:
                w1 = sbg(es, "w1", [128, 8, HID], BF16)
                w2 = sbg(es, "w2", [128, 32, D], BF16)
                xbs = [sbg(es, "xbB%d" % i, [128, 8, TB], F32) for i in range(2)]
                hTs = [sbg(es, "hTB%d" % i, [128, 8, TB], BF16) for i in range(2)]
                sqs = sbg(es, "sqsB", [128, 8, TB], BF16)
                hid = sbg(es, "hid", [128, 32, TB], BF16)
                yTs = [sbg(es, "yTB%d" % i, [128, 8, TB], F32) for i in range(2)]
                rstd = sbg(es, "rstdB", [128, TB], F32)
                rstd2 = sbg(es, "rstdB2", [128, TB], F32)
                sqt = [sbg(es, "sqtB%d" % i, [128, TB], F32) for i in range(2)]
                stg = sbg(es, "stgB", [128, 1024], F32)
                for kc in range(8):
                    for hf in range(4):
                        load_cast(w1[:, kc, hf * 1024:(hf + 1) * 1024], ("w1", kc),
                                  w1d[l, kc * 128:(kc + 1) * 128, hf * 1024:(hf + 1) * 1024], stg, "stgB", 1024, 4 * kc + hf)
                for g in range(32):
                    load_cast(w2[:, g, :], ("w2", g // 4), w2d[l, 128 * g:128 * (g + 1), :], stg, "stgB", 1024, g)
                cnt = 0

                def prep(b):
                    X = xbs[b % 2]
                    xk = "xbB%d" % (b % 2)
                    S.dma("sp", X[:], dview(src, b), reads=[(srck, b)], writes=[xk])
                    rmsnorm(X, xk, hTs[b % 2], "hTB%d" % (b % 2), rstd, "rstdB", l, "nmlppre", 4 + (b % 2))

                prep(0)
                for b in range(NB):
                    X = xbs[b % 2]
                    xk = "xbB%d" % (b % 2)
                    Y = yTs[b % 2]
                    yk = "yTB%d" % (b % 2)
                    hT = hTs[b % 2]
                    hk = "hTB%d" % (b % 2)
                    for hc in range(32):
                        ps, pk = half(cnt % 4)
                        sq = sqt[cnt % 2]
                        sqk = "sqtB%d" % (cnt % 2)
                        cnt += 1
                        for kc in range(8):
                            mm(ps, pk, w1[:, kc, hc * 128:(hc + 1) * 128], hT[:, kc, :], [("w1", kc), hk], start=(kc == 0),
                               stop=(kc == 7))
                        S.add("act", (lambda ps, sq: lambda e: e.activation(out=sq[:], in_=ps, func=AF.Relu))(ps, sq),
                              [pk], [sqk])
                        S.add("dve", (lambda sq, hc: lambda e: e.tensor_tensor(
                            out=hid[:, hc, :], in0=sq[:], in1=sq[:], op=ALU.mult))(sq, hc),
                            [sqk], [("hid", hc)])
                    if b + 1 < NB:
                        prep(b + 1)
                    for oc in range(8):
                        ps, pk = half(cnt % 4)
                        cnt += 1
                        for kc in range(32):
                            mm(ps, pk, w2[:, kc, oc * 128:(oc + 1) * 128], hid[:, kc, :], [("w2", kc // 4), ("hid", kc)],
                               start=(kc == 0), stop=(kc == 31))
                        S.add("act", (lambda ps, oc, Y: lambda e: e.copy(out=Y[:, oc, :], in_=ps))(ps, oc, Y), [pk], [yk])
                    postnorm_residual(Y, yk, X, xk, sqs, "sqsB", rstd2, "rstdB2", l, "nmlppost", 6 + (b % 2))
                    S.dma("sp", dview(dst, b), Y[:], reads=[yk], writes=[(dstk, b)])
                S.flush()

        def phase_mix(l, src, srck, dst, dstk):
            from_mix = _phase_mix_impl
            from_mix(l, src, srck, dst, dstk)

        MIXERS = os.environ.get("KMIX", "ALHR")
        KR = int(os.environ.get("KR", "0"))

        def _phase_mix_impl(l, src, srck, dst, dstk):
            with ExitStack() as es:
                win = sbg(es, "win", [128, 8, DIN], BF16)
                wout = sbg(es, "wout", [128, 8, D], BF16)
                stgs = [sbg(es, "stgA%d" % i, [128, DIN // 8], F32) for i in range(2)]
                stg = stgs[0]
                for kc in range(8):
                    for hf in range(8):
                        cs_ = slice(hf * (DIN // 8), (hf + 1) * (DIN // 8))
                        load_cast(win[:, kc, cs_], ("win", kc), w_in[l, kc * 128:(kc + 1) * 128, cs_], stgs[hf % 2],
                                  "stgA%d" % (hf % 2), DIN // 8, 8 * kc + hf)
                for kc in range(8):
                    for hf in range(4):
                        load_cast(wout[:, kc, hf * 256:(hf + 1) * 256], ("wout", kc),
                                  w_out[l, kc * 128:(kc + 1) * 128, hf * 256:(hf + 1) * 256], stgs[hf % 2], "stgA%d" % (hf % 2), 256, 4 * kc + hf)
                biasT = sbg(es, "biasT", [128, 5, 4, 128], F32)
                S.dma("sp", biasT[:], bias_d[:, l], writes=["biasT"])
                lr3 = sbg(es, "lr3", [128, 3, 256], F32)
                S.dma("sp", lr3[:], lr3_d[:, l], writes=["lr3"])
                lruw = sbg(es, "lruw", [128, 2, 2, 128], F32)
                S.dma("sp", lruw[:], lruw_d[:, l], writes=["lruw"])
                xb = sbg(es, "xbA", [128, 8, TB], F32)
                hT = sbg(es, "hTA", [128, 8, TB], BF16)
                rstd = sbg(es, "rstdA", [128, TB], F32)
                mixT = sbg(es, "mixT", [128, 8, TB], BF16)
                Gbig = sbg(es, "Gbig", [128, 14, 2, TB], F32)
                G = [Gbig[:, i] for i in range(14)]
                Gk = ["G%d" % i for i in range(14)]
                yT = Gbig[:, 10:14].rearrange("p a b t -> p (a b) t")
                S.alias["yTA"] = [Gk[10], Gk[11], Gk[12], Gk[13]]
                xe = sbg(es, "xe", [128, 2, 3 + TB], F32)
                hprev = sbg(es, "hprev", [128, 2], F32)
                S.add("pool", lambda e: e.memset(xe[:], 0.0), [], ["xe"])
                S.add("pool", lambda e: e.memset(hprev[:], 0.0), [], ["hprev"])
                S.add("pool", lambda e: e.memset(mixT[:], 0.0), [], ["mixT"])
                kz = sbg(es, "kz", [128, 2, 2, 8, 128], BF16)
                vring = sbg(es, "vring", [128, 8, 256], BF16)
                qT = sbg(es, "qT", [128, 2, TB], BF16)
                s_sb = sbg(es, "s_sb", [128, 4, 128], F32)
                pT = sbg(es, "pT", [128, 5, 4, 128], BF16)
                rden = sbg(es, "rden", [128, 2, 128], F32)
                S.add("pool", lambda e: e.memset(kz[:], 0.0), [], ["kz"])
                S.add("pool", lambda e: e.memset(vring[:], 0.0), [], ["vring"])
                dcnt = [0]

                def inproj(c, dst_ap, dstk):
                    ps, pk = half(dcnt[0] % 2)
                    dcnt[0] += 1
                    for kc in range(8):
                        mm(ps, pk, win[:, kc, c * 128:(c + 1) * 128], hT[:, kc, :], [("win", kc), "hTA"], start=(kc == 0),
                           stop=(kc == 7))
                    S.add("act", lambda e: e.copy(out=dst_ap, in_=ps), [pk], [dstk])

                def mixer_lru(b):
                    gb, cv, gr, gi, aa, hs = G[0], G[1], G[2], G[3], G[4], G[5]
                    for ch in range(2):
                        inproj(21 + ch, xe[:, ch, 3:3 + TB], "xe")
                        inproj(23 + ch, gb[:, ch, :], Gk[0])
                        yield
                    for ch in range(2):
                        S.add("dve", (lambda ch: lambda e: e.tensor_scalar(
                            out=cv[:, ch, :], in0=xe[:, ch, 0:TB], scalar1=pv(l, "cw0", ch), scalar2=pv(l, "cb", ch),
                            op0=ALU.mult, op1=ALU.add))(ch), ["xe", "pvec"], [Gk[1]])
                        for j in range(1, 4):
                            S.add("dve", (lambda ch, j: lambda e: e.scalar_tensor_tensor(
                                out=cv[:, ch, :], in0=xe[:, ch, j:j + TB], scalar=pv(l, "cw%d" % j, ch), in1=cv[:, ch, :],
                                op0=ALU.mult, op1=ALU.add))(ch, j), ["xe", "pvec", Gk[1]], [Gk[1]])
                            yield
                    S.add("pool", lambda e: e.tensor_copy(out=xe[:, :, 0:3], in_=xe[:, :, TB:TB + 3]), ["xe"], ["xe"])
                    for ch in range(2):
                        for i, (dstt, dk, bn) in enumerate([(gr, Gk[2], "ba"), (gi, Gk[3], "bx")]):
                            ps, pk = half(2)
                            mm(ps, pk, lruw[:, i, ch, :], cv[:, ch, :], ["lruw", Gk[1]])
                            S.add("act", (lambda ps, dstt, ch, bn: lambda e: e.activation(
                                out=dstt[:, ch, :], in_=ps, func=AF.Sigmoid, bias=pv(l, bn, ch)))(ps, dstt, ch, bn),
                                [pk, "pvec"], [dk])
                            yield
                        S.add("act", (lambda ch: lambda e: e.activation(out=aa[:, ch, :], in_=gr[:, ch, :], func=AF.Exp,
                                                                        scale=nsp8[:, l, ch:ch + 1]))(ch),
                              [Gk[2], "nsp8"], [Gk[4]])
                    S.add("dve", lambda e: e.tensor_tensor(out=gr[:], in0=aa[:], in1=aa[:], op=ALU.mult), [Gk[4]], [Gk[2]])
                    S.add("dve", lambda e: e.tensor_scalar(out=gr[:], in0=gr[:], scalar1=-1.0, scalar2=1.0, op0=ALU.mult,
                                                           op1=ALU.add), [Gk[2]], [Gk[2]])
                    S.add("dve", lambda e: e.tensor_scalar(out=gr[:], in0=gr[:], scalar1=1e-30, scalar2=None,
                                                           op0=ALU.max), [Gk[2]], [Gk[2]])
                    yield
                    S.add("act", lambda e: e.activation(out=gr[:], in_=gr[:], func=AF.Sqrt), [Gk[2]], [Gk[2]])
                    yield
                    S.add("dve", lambda e: e.tensor_tensor(out=gi[:], in0=gi[:], in1=cv[:], op=ALU.mult), [Gk[3], Gk[1]], [Gk[3]])
                    S.add("dve", lambda e: e.tensor_tensor(out=gi[:], in0=gi[:], in1=gr[:], op=ALU.mult), [Gk[3], Gk[2]], [Gk[3]])
                    for ch in range(2):
                        S.add("dve", (lambda ch: lambda e: e.tensor_tensor_scan(
                            out=hs[:, ch, :], data0=aa[:, ch, :], data1=gi[:, ch, :], initial=hprev[:, ch:ch + 1],
                            op0=ALU.mult, op1=ALU.add))(ch), [Gk[4], Gk[3], "hprev"], [Gk[5]])
                        yield
                    S.add("pool", lambda e: e.tensor_copy(out=hprev[:], in_=hs[:, :, TB - 1]), [Gk[5]], ["hprev"])
                    S.add("act", lambda e: e.activation(out=gb[:], in_=gb[:], func=AF.Gelu_apprx_tanh), [Gk[0]], [Gk[0]])
                    S.add("dve", lambda e: e.tensor_tensor(out=mixT[:, 6:8, :], in0=hs[:], in1=gb[:], op=ALU.mult),
                          [Gk[5], Gk[0]], ["mixT"])

                def mixer_attn(b):
                    q, k, v = G[6], G[7], G[8]
                    for pr in range(2):
                        inproj(0 + pr, q[:, pr, :], Gk[6])
                        inproj(2 + pr, k[:, pr, :], Gk[7])
                        inproj(4 + pr, v[:, pr, :], Gk[8])
                        yield
                    S.add("act", lambda e: e.copy(out=qT[:], in_=q[:]), [Gk[6]], ["qT"])
                    s0 = (2 * b) % 8
                    for pr in range(2):
                        for h2 in range(2):
                            S.add("pool", (lambda pr, h2: lambda e: e.tensor_copy(
                                out=kz[64 * h2:64 * h2 + 64, pr, h2, s0:s0 + 2, :],
                                in_=k[64 * h2:64 * h2 + 64, pr, :].rearrange("p (s t) -> p s t", s=2)))(pr, h2),
                                [Gk[7]], ["kz"])
                    for tt in range(2):
                        pst = PS[7]
                        for pr in range(2):
                            S.add("pe", (lambda tt, pr: lambda e: e.transpose(
                                out=pst[:, pr * 128:(pr + 1) * 128], in_=v[:, pr, tt * 128:(tt + 1) * 128],
                                identity=C("I")))(tt, pr), [Gk[8], "consts"], ["ps7"])
                        S.add("act", (lambda tt: lambda e: e.copy(out=vring[:, s0 + tt, :], in_=pst[:, 0:256]))(tt),
                              ["ps7"], ["vring"])
                        yield
                    for tt in range(2):
                        m = 2 * b + tt
                        js = [j for j in range(5) if m - 4 + j >= 0]
                        for j in js:
                            slot = (m - 4 + j) % 8
                            bank = 3 + (j % 2)
                            ps = PS[bank]
                            pk = "ps%d" % bank
                            for h in range(4):
                                mm(ps[:, h * 128:(h + 1) * 128], pk, kz[:, h // 2, h % 2, slot, :],
                                   qT[:, h // 2, tt * 128:(tt + 1) * 128], ["kz", "qT"])
                            S.add("dve", (lambda ps, j: lambda e: e.scalar_tensor_tensor(
                                out=s_sb[:], in0=ps[:].rearrange("p (h q) -> p h q", h=4), scalar=0.125,
                                in1=biasT[:, j, :, :], op0=ALU.mult, op1=ALU.add))(ps, j), [pk, "biasT"], ["s_sb"])
                            S.add("act", (lambda j: lambda e: e.activation(out=pT[:, j, :, :], in_=s_sb[:], func=AF.Exp))(j),
                                  ["s_sb"], [("pT", j)])
                            yield
                        for (pst2, pk2, which) in [(PS[5], "ps5", 0), (PS[6], "ps6", 1)]:
                            for h in range(4):
                                for j in js:
                                    slot = (m - 4 + j) % 8
                                    lhs = vring[:, slot, h * 64:(h + 1) * 64] if which == 0 else ones64[:]
                                    mm(pst2[64 * (h % 2):64 * (h % 2) + 64, (h // 2) * 128:(h // 2) * 128 + 128], pk2, lhs,
                                       pT[:, j, h, :], ["vring", "ones64", ("pT", j)], start=(j == js[0]), stop=(j == js[-1]))
                                yield
                        S.add("dve", lambda e: e.reciprocal(out=rden[:], in_=PS[6][:, 0:256].rearrange("p (a q) -> p a q", a=2)),
                              ["ps6"], ["rden"])
                        S.add("dve", (lambda tt: lambda e: e.tensor_tensor(
                            out=mixT[:, 0:2, tt * 128:(tt + 1) * 128],
                            in0=PS[5][:, 0:256].rearrange("p (a q) -> p a q", a=2), in1=rden[:], op=ALU.mult))(tt),
                            ["ps5", "rden"], ["mixT"])

                Z = [sbg(es, "Z%d" % i, [128, 2, NCK, 2, 64], BF16) for i in range(7)]
                Zk = ["Z%d" % i for i in range(7)]
                for i in range(7):
                    S.add("pool", (lambda i: lambda e: e.memset(Z[i][:], 0.0))(i), [], [Zk[i]])
                STh = sbg(es, "STh", [128, 2, 64], F32)
                STbdh = sbg(es, "STbdh", [128, 2, 128], BF16)
                Qtb = sbg(es, "Qtb", [128, 2, TB], BF16)
                Qab = sbg(es, "Qab", [128, 2, TB], BF16)
                S.add("pool", lambda e: e.memset(STh[:], 0.0), [], [("STh", 0), ("STh", 1)])
                S.add("pool", lambda e: e.memset(STbdh[:], 0.0), [], [("STbdh", 0), ("STbdh", 1)])
                dl = sbg(es, "dl", [128, 2 * NCK], F32)
                Mt = [sbg(es, "Mt%d" % i, [128, 128], F32 if i in (0, 1, 2, 3, 4, 12) else BF16) for i in range(15)]
                Mk = ["Mt%d" % i for i in range(15)]
                LANE_IDX = (0, 1, 2, 3, 4, 5, 6, 7, 8, 9, 10, 11)
                LMt = [Mt]
                LMk = [Mk]
                for ln in range(1, NCK):
                    tl = list(Mt)
                    kl = list(Mk)
                    for i in LANE_IDX:
                        tl[i] = sbg(es, "Mt%d_l%d" % (i, ln), [128, 128], F32 if i in (0, 1, 2, 3, 4) else BF16)
                        kl[i] = "Mt%d_l%d" % (i, ln)
                    LMt.append(tl)
                    LMk.append(kl)

                def run_lanes(gens):
                    active = list(gens)
                    while active:
                        for g_ in list(active):
                            try:
                                next(g_)
                            except StopIteration:
                                active.remove(g_)

                def half_copy(eng, dstZ, dk, srcT, sk, fn=None):
                    for h2 in range(2):
                        rows = slice(64 * h2, 64 * h2 + 64)
                        S.add(eng, (lambda rows, h2: lambda e: e.tensor_copy(
                            out=dstZ[rows, :, :, h2, :], in_=srcT[rows, :, :].rearrange("p a (c t) -> p a c t", c=NCK)))(rows, h2),
                            [sk], [dk])

                def half_mul(dstZ, dk, aT, ak, bT, bk):
                    for h2 in range(2):
                        rows = slice(64 * h2, 64 * h2 + 64)
                        S.add("dve", (lambda rows, h2: lambda e: e.tensor_tensor(
                            out=dstZ[rows, :, :, h2, :], in0=aT[rows, :, :].rearrange("p a (c t) -> p a c t", c=NCK),
                            in1=bT[rows, :, :].rearrange("p a (c t) -> p a c t", c=NCK), op=ALU.mult))(rows, h2),
                            [ak, bk], [dk])

                def cview(t):
                    return t[:].rearrange("p a (c t) -> p (a c) t", c=NCK)

                def mixer_hgrn(b):
                    q, f, iv, g, cum, key, t6, Qt, Qa, t9 = G[0], G[1], G[2], G[3], G[4], G[5], G[6], G[7], G[8], G[9]
                    Khz, Vz, Ktz = Z[0], Z[1], Z[2]
                    for pr in range(2):
                        inproj(6 + pr, q[:, pr, :], Gk[0])
                        inproj(8 + pr, f[:, pr, :], Gk[1])
                        inproj(10 + pr, iv[:, pr, :], Gk[2])
                        inproj(12 + pr, g[:, pr, :], Gk[3])
                    if "R" in MIXERS:
                        for i in range(7):
                            inproj(14 + i, pcx[:, i, 1:1 + TB], "pcx")
                    S.add("act", lambda e: e.activation(out=f[:], in_=f[:], func=AF.Sigmoid), [Gk[1]], [Gk[1]])
                    for pr in range(2):
                        S.add("dve", (lambda pr: lambda e: e.tensor_scalar(
                            out=f[:, pr, :], in0=f[:, pr, :], scalar1=omlb[:, l, pr:pr + 1], scalar2=lbv[:, l, pr:pr + 1],
                            op0=ALU.mult, op1=ALU.add))(pr), [Gk[1], "omlb", "lbv"], [Gk[1]])
                    S.add("dve", lambda e: e.tensor_scalar(out=key[:], in0=f[:], scalar1=-1.0, scalar2=1.0, op0=ALU.mult,
                                                           op1=ALU.add), [Gk[1]], [Gk[5]])
                    S.add("act", lambda e: e.activation(out=f[:], in_=f[:], func=AF.Ln), [Gk[1]], [Gk[1]])
                    for pr in range(2):
                        S.add("dve", (lambda pr: lambda e: e.tensor_tensor_scan(
                            out=cum[:, pr, :], data0=C("cm"), data1=f[:, pr, :], initial=0.0, op0=ALU.mult, op1=ALU.add))(pr),
                            [Gk[1], "consts"], [Gk[4]])
                    S.add("act", lambda e: e.activation(out=q[:], in_=q[:], func=AF.Silu), [Gk[0]], [Gk[0]])
                    cv_ = cview(cum)
                    S.add("dve", lambda e: e.tensor_tensor(out=cview(t6), in0=cv_, in1=cv_[:, :, 31:32].to_broadcast([128, 2 * NCK, 64]),
                                                           op=ALU.subtract), [Gk[4]], [Gk[6]])
                    S.add("act", lambda e: e.activation(out=t9[:], in_=t6[:], func=AF.Exp), [Gk[6]], [Gk[9]])
                    S.add("dve", lambda e: e.tensor_tensor(out=Qtb[:], in0=q[:], in1=t9[:], op=ALU.mult), [Gk[0], Gk[9]], ["Qtb"])
                    S.add("act", lambda e: e.activation(out=t9[:], in_=t6[:], func=AF.Exp, scale=-1.0), [Gk[6]], [Gk[9]])
                    half_mul(Khz, Zk[0], key, Gk[5], t9, Gk[9])
                    S.add("act", lambda e: e.activation(out=t9[:], in_=cum[:], func=AF.Exp), [Gk[4]], [Gk[9]])
                    S.add("dve", lambda e: e.tensor_tensor(out=Qab[:], in0=q[:], in1=t9[:], op=ALU.mult), [Gk[0], Gk[9]], ["Qab"])
                    S.add("dve", lambda e: e.tensor_tensor(out=cview(t6), in0=cv_, in1=cv_[:, :, 63:64].to_broadcast([128, 2 * NCK, 64]),
                                                           op=ALU.subtract), [Gk[4]], [Gk[6]])
                    S.add("act", lambda e: e.activation(out=t9[:], in_=t6[:], func=AF.Exp, scale=-1.0), [Gk[6]], [Gk[9]])
                    half_mul(Ktz, Zk[2], key, Gk[5], t9, Gk[9])
                    S.add("act", lambda e: e.activation(out=dl[:], in_=cv_[:, :, 63], func=AF.Exp), [Gk[4]], ["dl"])
                    half_copy("pool", Vz, Zk[1], iv, Gk[2])
                    S.add("act", lambda e: e.activation(out=g[:], in_=g[:], func=AF.Silu), [Gk[3]], [Gk[3]])
                    def hlane(pr, c, li):
                        ATs, Vbd, Vst, Ktbd = LMt[li][5], LMt[li][8], LMt[li][9], LMt[li][10]
                        AK, VbK, VsK, KtK = LMk[li][5], LMk[li][8], LMk[li][9], LMk[li][10]
                        cs = slice(c * 64, c * 64 + 64)
                        zK = Khz[:, pr, c, :, :].rearrange("p a t -> p (a t)")
                        zV = Vz[:, pr, c, :, :].rearrange("p a t -> p (a t)")
                        zT = Ktz[:, pr, c, :, :].rearrange("p a t -> p (a t)")
                        ps, pk = nb()
                        mm(ps[:, 0:64], pk, zK, Qtb[:, pr, cs], [Zk[0], "Qtb"])
                        S.add("dve", (lambda ps: lambda e: e.tensor_tensor(out=ATs[:, 0:64], in0=ps[:, 0:64], in1=C("iu"), op=ALU.mult))(ps),
                              [pk, "consts"], [AK])
                        yield
                        ps, pk = nb()
                        mm(ps[:, 0:128], pk, zV, Cb("I"), [Zk[1], "constsb"])
                        S.add("act", (lambda ps: lambda e: e.copy(out=Vbd[:], in_=ps[:, 0:128]))(ps), [pk], [VbK])
                        yield
                        ps, pk = nb()
                        mm(ps[:, 0:64], pk, zV, Cb("E2"), [Zk[1], "constsb"])
                        S.add("act", (lambda ps: lambda e: e.copy(out=Vst[:, 0:64], in_=ps[:, 0:64]))(ps), [pk], [VsK])
                        yield
                        ps, pk = nb()
                        mm(ps[:, 0:128], pk, zT, Cb("I"), [Zk[2], "constsb"])
                        S.add("dve", (lambda ps: lambda e: e.tensor_copy(out=Ktbd[:], in_=ps[:, 0:128]))(ps), [pk], [KtK])
                        yield

                    PSO = [(PS[2], "ps2"), (PS[7], "ps7")]

                    def hseq(pr, ccs):
                        psO, psOk = PSO[pr]
                        for c in ccs:
                            li = pr * 2 + c % 2
                            ATs, Vbd, Vst, Ktbd = LMt[li][5], LMt[li][8], LMt[li][9], LMt[li][10]
                            AK, VbK, VsK, KtK = LMk[li][5], LMk[li][8], LMk[li][9], LMk[li][10]
                            cs = slice(c * 64, c * 64 + 64)
                            mm(psO[:, cs], psOk, Vbd[:], ATs[:, 0:64], [VbK, AK], start=True, stop=False)
                            mm(psO[:, cs], psOk, STbdh[:, pr, :], Qab[:, pr, cs], [("STbdh", pr), "Qab"], start=False, stop=True)
                            yield
                            ps, pk = nb()
                            mm(ps[:, 0:64], pk, Ktbd[:], Vst[:, 0:64], [KtK, VsK])
                            S.add("dve", (lambda ps, pr, c: lambda e: e.scalar_tensor_tensor(
                                out=STh[:, pr, :], in0=STh[:, pr, :], scalar=dl[:, pr * NCK + c:pr * NCK + c + 1],
                                in1=ps[:, 0:64], op0=ALU.mult, op1=ALU.add))(ps, pr, c), [("STh", pr), "dl", pk], [("STh", pr)])
                            yield
                            for h2 in range(2):
                                rows = slice(64 * h2, 64 * h2 + 64)
                                S.add("act", (lambda rows, h2, pr: lambda e: e.copy(
                                    out=STbdh[rows, pr, 64 * h2:64 * h2 + 64], in_=STh[rows, pr, :]))(rows, h2, pr),
                                    [("STh", pr)], [("STbdh", pr)])
                            yield

                    for hf in range(NCK // 2):
                        ccs = (2 * hf, 2 * hf + 1)
                        run_lanes([hlane(pr, c, pr * 2 + c % 2) for pr in range(2) for c in ccs])
                        run_lanes([hseq(0, ccs), hseq(1, ccs)]) if os.environ.get('KZIP', '1') == '1' else (run_lanes([hseq(0, ccs)]), run_lanes([hseq(1, ccs)]))
                    def hpost(pr, psO, psOk):
                        S.add("act", (lambda pr: lambda e: e.activation(out=t6[:, 0, :], in_=psO[:, 0:TB], func=AF.Square))(pr),
                              [psOk], [Gk[6]])
                        mm(PS[3][:, 0:TB], "ps3", C("bd64"), t6[:, 0, :], ["consts", Gk[6]])
                        S.add("act", lambda e: e.activation(out=t6[:, 1, :], in_=PS[3][:, 0:TB], func=AF.Sqrt, bias=epsc[:, 0:1]),
                              ["ps3", "epsc"], [Gk[6]])
                        S.add("dve", lambda e: e.reciprocal(out=t6[:, 1, :], in_=t6[:, 1, :]), [Gk[6]], [Gk[6]])
                        S.add("dve", lambda e: e.tensor_tensor(out=t6[:, 0, :], in0=psO[:, 0:TB], in1=t6[:, 1, :], op=ALU.mult),
                              [psOk, Gk[6]], [Gk[6]])
                        S.add("dve", (lambda pr: lambda e: e.scalar_tensor_tensor(
                            out=mixT[:, 2 + pr, :], in0=g[:, pr, :], scalar=pv(l, "hnorm", pr), in1=t6[:, 0, :],
                            op0=ALU.mult, op1=ALU.mult))(pr), [Gk[3], "pvec", Gk[6]], ["mixT"])

                    for pr in range(2):
                        hpost(pr, PSO[pr][0], PSO[pr][1])

                pcx = sbg(es, "pcx", [128, 7, 1 + TB], F32)
                STr = sbg(es, "STr", [128, 2, 64], F32)
                STbdr = sbg(es, "STbdr", [128, 2, 128], BF16)
                STrb = sbg(es, "STrb", [128, 2, 64], BF16)
                S.add("pool", lambda e: e.memset(STrb[:], 0.0), [], [("STrb", 0), ("STrb", 1)])
                S.add("pool", lambda e: e.memset(pcx[:], 0.0), [], ["pcx"])
                S.add("pool", lambda e: e.memset(STr[:], 0.0), [], [("STr", 0), ("STr", 1)])
                S.add("pool", lambda e: e.memset(STbdr[:], 0.0), [], [("STbdr", 0), ("STbdr", 1)])
                S.add("pool", lambda e: e.memset(Mt[14][:], 0.0), [], [Mk[14]])
                WUU = [(Mt[12], Mt[13], Mt[14], Mk[12], Mk[13], Mk[14])]
                _w1 = sbg(es, "Wsb1", [128, 128], F32)
                _u1 = sbg(es, "Ust1", [128, 128], BF16)
                _b1 = sbg(es, "Ubd1", [128, 128], BF16)
                S.add("pool", lambda e: e.memset(_b1[:], 0.0), [], ["Ubd1"])
                WUU.append((_w1, _u1, _b1, "Wsb1", "Ust1", "Ubd1"))
                rb = [0]
                RBANKS = [3, 4, 5, 6]

                def nb():
                    i = RBANKS[rb[0] % len(RBANKS)]
                    rb[0] += 1
                    return PS[i], "ps%d" % i

                def mixer_rwkv(b):
                    r, k, v, lo, lw, a, g, kkn, bb, c, t10, t11, Rt, Ra = G
                    (rK, kK, vK, loK, lwK, aK, gK, kknK, bbK, cK, t10K, t11K, RtK, RaK) = Gk
                    Khz, Vz, Ktz, Bhz, Qtz, Qaz, Btz = Z
                    if "H" not in MIXERS:
                        for i in range(7):
                            inproj(14 + i, pcx[:, i, 1:1 + TB], "pcx")
                    dsts = [(r, rK, 0), (r, rK, 1), (k, kK, 0), (k, kK, 1), (v, vK, 0), (v, vK, 1), (lo, loK, 0)]
                    for i, (dt_, dk_, j) in enumerate(dsts):
                        S.add("dve", (lambda i: lambda e: e.tensor_tensor(out=t10[:, 0, :], in0=pcx[:, i, 0:TB],
                                                                          in1=pcx[:, i, 1:1 + TB], op=ALU.subtract))(i),
                              ["pcx"], [t10K])
                        S.add("dve", (lambda i, dt_, j: lambda e: e.scalar_tensor_tensor(
                            out=dt_[:, j, :], in0=t10[:, 0, :], scalar=pv(l, "mu", i), in1=pcx[:, i, 1:1 + TB],
                            op0=ALU.mult, op1=ALU.add))(i, dt_, j), [t10K, "pcx", "pvec"], [dk_])
                    S.add("pool", lambda e: e.tensor_copy(out=pcx[:, :, 0:1], in_=pcx[:, :, TB:TB + 1]), ["pcx"], ["pcx"])
                    if KR == 1:
                        return
                    S.add("act", lambda e: e.activation(out=lo[0:32, 1, :], in_=lo[0:32, 0, :], func=AF.Tanh), [loK], [loK])
                    S.add("dve", lambda e: e.tensor_copy(out=lo[32:64, 1, :], in_=lo[32:64, 0, :]), [loK], [loK])
                    S.add("act", lambda e: e.activation(out=lo[64:128, 1, :], in_=lo[64:128, 0, :], func=AF.Sigmoid), [loK], [loK])
                    for pr in range(2):
                        ps, pk = nb()
                        mm(ps[:, 0:TB], pk, lr3[:, 0, pr * 128:(pr + 1) * 128], lo[:, 1, :], ["lr3", loK])
                        S.add("act", (lambda ps, pr: lambda e: e.activation(out=lw[:, pr, :], in_=ps[:, 0:TB], func=AF.Sigmoid,
                                                                            bias=pv(l, "w0", pr)))(ps, pr), [pk, "pvec"], [lwK])
                        ps, pk = nb()
                        mm(ps[:, 0:TB], pk, lr3[:, 1, pr * 128:(pr + 1) * 128], lo[:, 1, :], ["lr3", loK])
                        S.add("act", (lambda ps, pr: lambda e: e.activation(out=a[:, pr, :], in_=ps[:, 0:TB], func=AF.Sigmoid,
                                                                            bias=pv(l, "a0", pr)))(ps, pr), [pk, "pvec"], [aK])
                        ps, pk = nb()
                        mm(ps[:, 0:TB], pk, lr3[:, 2, pr * 128:(pr + 1) * 128], lo[:, 1, :], ["lr3", loK])
                        S.add("act", (lambda ps, pr: lambda e: e.copy(out=g[:, pr, :], in_=ps[:, 0:TB]))(ps, pr), [pk], [gK])
                    S.add("dve", lambda e: e.tensor_scalar(out=lw[:], in0=lw[:], scalar1=-0.6065306597126334, scalar2=None,
                                                           op0=ALU.mult), [lwK], [lwK])
                    if KR == 2:
                        return
                    for pr in range(2):
                        S.add("dve", (lambda pr: lambda e: e.tensor_scalar(out=kkn[:, pr, :], in0=k[:, pr, :],
                                                                           scalar1=pv(l, "kk", pr), scalar2=None,
                                                                           op0=ALU.mult))(pr), [kK, "pvec"], [kknK])
                    S.add("act", lambda e: e.activation(out=t10[:], in_=kkn[:], func=AF.Square), [kknK], [t10K])
                    for pr in range(2):
                        ps, pk = nb()
                        mm(ps[:, 0:TB], pk, C("bd1"), t10[:, pr, :], ["consts", t10K])
                        S.add("act", (lambda ps, pr: lambda e: e.activation(out=t11[:, pr, :], in_=ps[:, 0:TB], func=AF.Sqrt,
                                                                            bias=epsc[:, 2:3]))(ps, pr), [pk, "epsc"], [t11K])
                    S.add("dve", lambda e: e.reciprocal(out=t11[:], in_=t11[:]), [t11K], [t11K])
                    S.add("dve", lambda e: e.tensor_tensor(out=kkn[:], in0=kkn[:], in1=t11[:], op=ALU.mult), [kknK, t11K], [kknK])
                    for pr in range(2):
                        S.add("dve", (lambda pr: lambda e: e.tensor_scalar(out=t10[:, pr, :], in0=a[:, pr, :], scalar1=epsc[:, 3:4],
                                                                           scalar2=pv(l, "ka", pr), op0=ALU.add,
                                                                           op1=ALU.mult))(pr), [aK, "pvec", "epsc"], [t10K])
                    S.add("dve", lambda e: e.scalar_tensor_tensor(out=k[:], in0=t10[:], scalar=1.0, in1=k[:], op0=ALU.add,
                                                                  op1=ALU.mult), [t10K, kK], [kK])
                    S.add("dve", lambda e: e.tensor_tensor(out=bb[:], in0=kkn[:], in1=a[:], op=ALU.mult), [kknK, aK], [bbK])
                    for pr in range(2):
                        S.add("dve", (lambda pr: lambda e: e.tensor_tensor_scan(
                            out=c[:, pr, :], data0=C("cm"), data1=lw[:, pr, :], initial=0.0, op0=ALU.mult, op1=ALU.add))(pr),
                            [lwK, "consts"], [cK])
                    cv_ = cview(c)
                    S.add("act", lambda e: e.activation(out=dl[:], in_=cv_[:, :, 63], func=AF.Exp), [cK], ["dl"])
                    S.add("dve", lambda e: e.tensor_tensor(out=cview(t10), in0=cv_, in1=cv_[:, :, 31:32].to_broadcast([128, 2 * NCK, 64]),
                                                           op=ALU.subtract), [cK], [t10K])
                    S.add("act", lambda e: e.activation(out=t11[:], in_=t10[:], func=AF.Exp), [t10K], [t11K])
                    S.add("dve", lambda e: e.tensor_tensor(out=Qtb[:], in0=r[:], in1=t11[:], op=ALU.mult), [rK, t11K], ["Qtb"])
                    S.add("act", lambda e: e.activation(out=t11[:], in_=t10[:], func=AF.Exp, scale=-1.0), [t10K], [t11K])
                    half_mul(Khz, Zk[0], k, kK, t11, t11K)
                    half_mul(Bhz, Zk[3], bb, bbK, t11, t11K)
                    S.add("dve", lambda e: e.tensor_tensor(out=t10[:], in0=t10[:], in1=lw[:], op=ALU.subtract), [t10K, lwK], [t10K])
                    S.add("act", lambda e: e.activation(out=t11[:], in_=t10[:], func=AF.Exp), [t10K], [t11K])
                    half_mul(Qtz, Zk[4], kkn, kknK, t11, t11K)
                    S.add("act", lambda e: e.activation(out=t11[:], in_=c[:], func=AF.Exp), [cK], [t11K])
                    S.add("dve", lambda e: e.tensor_tensor(out=Qab[:], in0=r[:], in1=t11[:], op=ALU.mult), [rK, t11K], ["Qab"])
                    S.add("dve", lambda e: e.tensor_tensor(out=t10[:], in0=c[:], in1=lw[:], op=ALU.subtract), [cK, lwK], [t10K])
                    S.add("act", lambda e: e.activation(out=t11[:], in_=t10[:], func=AF.Exp), [t10K], [t11K])
                    half_mul(Qaz, Zk[5], kkn, kknK, t11, t11K)
                    S.add("dve", lambda e: e.tensor_tensor(out=cview(t10), in0=cv_, in1=cv_[:, :, 63:64].to_broadcast([128, 2 * NCK, 64]),
                                                           op=ALU.subtract), [cK], [t10K])
                    S.add("act", lambda e: e.activation(out=t11[:], in_=t10[:], func=AF.Exp, scale=-1.0), [t10K], [t11K])
                    half_mul(Ktz, Zk[2], k, kK, t11, t11K)
                    half_mul(Btz, Zk[6], bb, bbK, t11, t11K)
                    half_copy("pool", Vz, Zk[1], v, vK)
                    if KR == 3:
                        return
                    Wsb, Ust, Ubd = Mt[12], Mt[13], Mt[14]
                    WsbK, UstK, UbdK = Mk[12], Mk[13], Mk[14]

                    def z2(zt, pr, cc):
                        return zt[:, pr, cc, :, :].rearrange("p a t -> p (a t)")

                    def rlane(pr, cc, li):
                        (X0, XT0, X1, XT1, Am, MkkT, Ark, Arb, Vbd, Vst, Ktbd, nBtbd) = LMt[li][0:12]
                        (X0K, XT0K, X1K, XT1K, AmK, MkkTK, ArkK, ArbK, VbdK, VstK, KtbdK, nBtbdK) = LMk[li][0:12]
                        cs = slice(cc * 64, cc * 64 + 64)
                        ps, pk = nb()
                        mm(ps[:, 0:128], pk, z2(Bhz, pr, cc), z2(Qtz, pr, cc), [Zk[3], Zk[4]])
                        S.add("dve", (lambda ps: lambda e: e.tensor_tensor(out=X0[:], in0=ps[:, 0:128], in1=C("nsu"), op=ALU.mult))(ps),
                              [pk, "consts"], [X0K])
                        yield
                        ps, pk = nb()
                        mm(ps[:, 0:128], pk, z2(Qtz, pr, cc), z2(Bhz, pr, cc), [Zk[3], Zk[4]])
                        S.add("dve", (lambda ps: lambda e: e.tensor_tensor(out=XT0[:], in0=ps[:, 0:128], in1=C("nsl"), op=ALU.mult))(ps),
                              [pk, "consts"], [XT0K])
                        yield
                        ps, pk = nb()
                        mm(ps[:, 0:128], pk, z2(Khz, pr, cc), z2(Qtz, pr, cc), [Zk[0], Zk[4]])
                        S.add("dve", (lambda ps: lambda e: e.tensor_tensor(out=MkkT[:], in0=ps[:, 0:128], in1=C("su"), op=ALU.mult))(ps),
                              [pk, "consts"], [MkkTK])
                        yield
                        ps, pk = nb()
                        mm(ps[:, 0:64], pk, z2(Khz, pr, cc), Qtb[:, pr, cs], [Zk[0], "Qtb"])
                        S.add("dve", (lambda ps: lambda e: e.tensor_tensor(out=Ark[:, 0:64], in0=ps[:, 0:64], in1=C("iu"), op=ALU.mult))(ps),
                              [pk, "consts"], [ArkK])
                        yield
                        ps, pk = nb()
                        mm(ps[:, 0:64], pk, z2(Bhz, pr, cc), Qtb[:, pr, cs], [Zk[3], "Qtb"])
                        S.add("dve", (lambda ps: lambda e: e.tensor_tensor(out=Arb[:, 0:64], in0=ps[:, 0:64], in1=C("niu"), op=ALU.mult))(ps),
                              [pk, "consts"], [ArbK])
                        yield
                        ps, pk = nb()
                        mm(ps[:, 0:128], pk, z2(Vz, pr, cc), Cb("I"), [Zk[1], "constsb"])
                        S.add("act", (lambda ps: lambda e: e.copy(out=Vbd[:], in_=ps[:, 0:128]))(ps), [pk], [VbdK])
                        yield
                        ps, pk = nb()
                        mm(ps[:, 0:64], pk, z2(Vz, pr, cc), Cb("E2"), [Zk[1], "constsb"])
                        S.add("act", (lambda ps: lambda e: e.copy(out=Vst[:, 0:64], in_=ps[:, 0:64]))(ps), [pk], [VstK])
                        yield
                        ps, pk = nb()
                        mm(ps[:, 0:128], pk, z2(Ktz, pr, cc), Cb("I"), [Zk[2], "constsb"])
                        S.add("act", (lambda ps: lambda e: e.copy(out=Ktbd[:], in_=ps[:, 0:128]))(ps), [pk], [KtbdK])
                        yield
                        ps, pk = nb()
                        mm(ps[:, 0:128], pk, z2(Btz, pr, cc), Cb("I"), [Zk[6], "constsb"])
                        S.add("act", (lambda ps: lambda e: e.mul(out=nBtbd[:], in_=ps[:, 0:128], mul=-1.0))(ps), [pk], [nBtbdK])
                        yield
                        S.add("dve", lambda e: e.tensor_tensor(out=Am[:], in0=X0[:], in1=C("I"), op=ALU.add), [X0K, "consts"], [AmK])
                        P, PK, PT, PTK = X0, X0K, XT0, XT0K
                        Pn, PnK, PTn, PTnK = X1, X1K, XT1, XT1K
                        for j in range(1, 6):
                            ps, pk = nb()
                            mm(ps[:, 0:128], pk, P[:], PT[:], [PK, PTK])
                            S.add("act", (lambda ps, PTn: lambda e: e.copy(out=PTn[:], in_=ps[:, 0:128]))(ps, PTn), [pk], [PTnK])
                            yield
                            if j <= 4:
                                ps2, pk2 = nb()
                                mm(ps2[:, 0:128], pk2, PT[:], P[:], [PK, PTK])
                                S.add("dve", (lambda ps2, Pn: lambda e: e.tensor_copy(out=Pn[:], in_=ps2[:, 0:128]))(ps2, Pn),
                                      [pk2], [PnK])
                                yield
                            ps3, pk3 = nb()
                            mm(ps3[:, 0:128], pk3, PTn[:], Am[:], [PTnK, AmK])
                            S.add("dve", (lambda ps3: lambda e: e.tensor_tensor(out=Am[:], in0=Am[:], in1=ps3[:, 0:128], op=ALU.add))(ps3),
                                  [pk3, AmK], [AmK])
                            yield
                            P, PK, PT, PTK, Pn, PnK, PTn, PTnK = Pn, PnK, PTn, PTnK, P, PK, PT, PTK

                    PSY = [(PS[2], "ps2"), (PS[7], "ps7")]

                    def rseq(pr, ccs):
                        psY, psYk = PSY[pr]
                        Wsb, Ust, Ubd, WsbK, UstK, UbdK = WUU[pr]
                        for cc in ccs:
                            li = pr * 2 + cc % 2
                            (X0, XT0, X1, XT1, Am, MkkT, Ark, Arb, Vbd, Vst, Ktbd, nBtbd) = LMt[li][0:12]
                            (X0K, XT0K, X1K, XT1K, AmK, MkkTK, ArkK, ArbK, VbdK, VstK, KtbdK, nBtbdK) = LMk[li][0:12]
                            cs = slice(cc * 64, cc * 64 + 64)
                            ps, pk = nb()
                            mm(ps[:, 0:64], pk, MkkT[:], Vst[:, 0:64], [MkkTK, VstK], start=True, stop=False)
                            mm(ps[:, 0:64], pk, z2(Qaz, pr, cc), STrb[:, pr, :], [Zk[5], ("STrb", pr)], start=False, stop=True)
                            S.add("act", (lambda ps: lambda e: e.copy(out=Wsb[:, 0:64], in_=ps[:, 0:64]))(ps), [pk], [WsbK])
                            yield
                            ps, pk = nb()
                            mm(ps[:, 0:64], pk, Am[:], Wsb[:, 0:64], [AmK, WsbK])
                            S.add("act", (lambda ps: lambda e: e.copy(out=Ust[:, 0:64], in_=ps[:, 0:64]))(ps), [pk], [UstK])
                            yield
                            for h2 in range(2):
                                rows = slice(64 * h2, 64 * h2 + 64)
                                S.add("act", (lambda rows, h2: lambda e: e.copy(
                                    out=Ubd[rows, 64 * h2:64 * h2 + 64], in_=Ust[rows, 0:64]))(rows, h2), [UstK], [UbdK])
                            yield
                            mm(psY[:, cs], psYk, STbdr[:, pr, :], Qab[:, pr, cs], [("STbdr", pr), "Qab"], start=True, stop=False)
                            mm(psY[:, cs], psYk, Vbd[:], Ark[:, 0:64], [VbdK, ArkK], start=False, stop=False)
                            mm(psY[:, cs], psYk, Ubd[:], Arb[:, 0:64], [UbdK, ArbK], start=False, stop=True)
                            yield
                            ps, pk = nb()
                            mm(ps[:, 0:64], pk, Ktbd[:], Vst[:, 0:64], [KtbdK, VstK], start=True, stop=False)
                            mm(ps[:, 0:64], pk, nBtbd[:], Ust[:, 0:64], [nBtbdK, UstK], start=False, stop=True)
                            S.add("dve", (lambda ps, pr, cc: lambda e: e.scalar_tensor_tensor(
                                out=STr[:, pr, :], in0=STr[:, pr, :], scalar=dl[:, pr * NCK + cc:pr * NCK + cc + 1],
                                in1=ps[:, 0:64], op0=ALU.mult, op1=ALU.add))(ps, pr, cc), [("STr", pr), "dl", pk], [("STr", pr)])
                            yield
                            for h2 in range(2):
                                rows = slice(64 * h2, 64 * h2 + 64)
                                S.add("act", (lambda rows, h2, pr: lambda e: e.copy(
                                    out=STbdr[rows, pr, 64 * h2:64 * h2 + 64], in_=STr[rows, pr, :]))(rows, h2, pr),
                                    [("STr", pr)], [("STbdr", pr)])
                            S.add("act", (lambda pr: lambda e: e.copy(out=STrb[:, pr, :], in_=STr[:, pr, :]))(pr),
                                  [("STr", pr)], [("STrb", pr)])
                            yield

                    for hf in range(NCK // 2):
                        ccs = (2 * hf, 2 * hf + 1)
                        run_lanes([rlane(pr, cc, pr * 2 + cc % 2) for pr in range(2) for cc in ccs])
                        run_lanes([rseq(0, ccs), rseq(1, ccs)]) if os.environ.get('KZIP', '1') == '1' else (run_lanes([rseq(0, ccs)]), run_lanes([rseq(1, ccs)]))
                    def rpost(pr, psY, psYk):
                        ysb2 = t10[:, 0, :]
                        yc2 = t10[:, 1, :]
                        sq2 = t11[:, 0, :]
                        rs2 = t11[:, 1, :]
                        S.add("act", lambda e: e.copy(out=ysb2, in_=psY[:, 0:TB]), [psYk], [t10K])
                        ps, pk = nb()
                        mm(ps[:, 0:TB], pk, C("bd64"), ysb2, ["consts", t10K])
                        S.add("dve", (lambda ps: lambda e: e.tensor_tensor(out=yc2, in0=ysb2, in1=ps[:, 0:TB], op=ALU.subtract))(ps),
                              [pk, t10K], [t10K])
                        S.add("act", lambda e: e.activation(out=sq2, in_=yc2, func=AF.Square), [t10K], [t11K])
                        ps, pk = nb()
                        mm(ps[:, 0:TB], pk, C("bd64"), sq2, ["consts", t11K])
                        S.add("act", (lambda ps: lambda e: e.activation(out=rs2, in_=ps[:, 0:TB], func=AF.Sqrt, bias=epsc[:, 1:2]))(ps),
                              [pk, "epsc"], [t11K])
                        S.add("dve", lambda e: e.reciprocal(out=rs2, in_=rs2), [t11K], [t11K])
                        S.add("dve", lambda e: e.tensor_tensor(out=yc2, in0=yc2, in1=rs2, op=ALU.mult), [t10K, t11K], [t10K])
                        S.add("dve", (lambda pr: lambda e: e.tensor_scalar(out=yc2, in0=yc2, scalar1=pv(l, "lnw", pr),
                                                                           scalar2=pv(l, "lnb", pr), op0=ALU.mult,
                                                                           op1=ALU.add))(pr), [t10K, "pvec"], [t10K])
                        S.add("dve", (lambda pr: lambda e: e.scalar_tensor_tensor(out=sq2, in0=r[:, pr, :], scalar=pv(l, "rk", pr),
                                                                                  in1=k[:, pr, :], op0=ALU.mult,
                                                                                  op1=ALU.mult))(pr), [rK, kK, "pvec"], [t11K])
                        ps, pk = nb()
                        mm(ps[:, 0:TB], pk, C("bd1"), sq2, ["consts", t11K])
                        S.add("dve", (lambda ps, pr: lambda e: e.tensor_tensor(out=sq2, in0=ps[:, 0:TB], in1=v[:, pr, :], op=ALU.mult))(ps, pr),
                              [pk, vK], [t11K])
                        S.add("dve", lambda e: e.tensor_tensor(out=yc2, in0=yc2, in1=sq2, op=ALU.add), [t10K, t11K], [t10K])
                        S.add("dve", (lambda pr: lambda e: e.tensor_tensor(out=mixT[:, 4 + pr, :], in0=yc2, in1=g[:, pr, :],
                                                                           op=ALU.mult))(pr), [t10K, gK], ["mixT"])

                    for pr in range(2):
                        rpost(pr, PSY[pr][0], PSY[pr][1])

                for b in range(NB):
                    S.dma("sp", xb[:], dview(src, b), reads=[(srck, b)], writes=["xbA"])
                    rmsnorm(xb, "xbA", hT, "hTA", rstd, "rstdA", l, "nmixpre", 2)
                    if "H" in MIXERS:
                        mixer_hgrn(b)
                    if "R" in MIXERS:
                        mixer_rwkv(b)
                    run_lanes(([mixer_attn(b)] if "A" in MIXERS else []) + ([mixer_lru(b)] if "L" in MIXERS else []))
                    if debug_mix and l == int(os.environ.get("KDBGL", "0")):
                        S.add("act", lambda e: e.copy(out=yT[:], in_=mixT[:]), ["mixT"], ["yTA"])
                        S.dma("sp", dview(mix_dbg, b), yT[:], reads=["yTA"], writes=[("mixdbg", b)])
                    for oc in range(8):
                        ps, pk = half(dcnt[0] % 2)
                        dcnt[0] += 1
                        for kc in range(8):
                            mm(ps, pk, wout[:, kc, oc * 128:(oc + 1) * 128], mixT[:, kc, :], [("wout", kc), "mixT"],
                               start=(kc == 0), stop=(kc == 7))
                        S.add("act", (lambda ps, oc: lambda e: e.copy(out=yT[:, oc, :], in_=ps))(ps, oc), [pk], ["yTA"])
                    postnorm_residual(yT, "yTA", xb, "xbA", hT, "hTA", rstd, "rstdA", l, "nmixpost", 2)
                    S.dma("sp", dview(dst, b), yT[:], reads=["yTA"], writes=[(dstk, b)])
                S.flush()

        KSTOP = int(os.environ.get("KSTOP", "0"))
        cur, curk = xT_in, "xin"
        for l in range(L if KSTOP != 1 else 0):
            if do_mix:
                phase_mix(l, cur, curk, scr[0], "scr0_%d" % l)
                cur, curk = scr[0], "scr0_%d" % l
            if do_mlp:
                last = (l == L - 1)
                dst, dstk = (yT_out, "yout") if last else (scr[1], "scr1_%d" % l)
                phase_mlp(l, cur, curk, dst, dstk)
                cur, curk = dst, dstk
        S.finish("sp")
        print("sched stats: ops", S.nops, "waits", S.nwait, "cnt", S.cnt, "ndma", len(S.dma_sigs), flush=True)
    return nc


def _feat(v):
    v = np.asarray(v, np.float32)
    return np.ascontiguousarray(v.reshape(-1, 128).T)


def pack_params(inp, L):
    pvec = np.zeros((128, L, NPV), np.float32)
    names = dict(nmixpre="norm_mix_pre", nmixpost="norm_mix_post", nmlppre="norm_mlp_pre", nmlppost="norm_mlp_post",
                 lblog="hgrn_lb_logits", hnorm="hgrn_norm", mu="rwkv_mu", w0="rwkv_w0", a0="rwkv_a0", kk="rwkv_k_k",
                 ka="rwkv_k_a", lnw="rwkv_ln_w", lnb="rwkv_ln_b", cb="lru_conv_b", ba="lru_ba", bx="lru_bx",
                 lam="lru_lambda")
    for l in range(L):
        for k, src in names.items():
            o, n = PVL[k]
            pvec[:, l, o:o + n] = _feat(inp[src][l])
        o, n = PVL["rk"]
        pvec[:, l, o:o + n] = _feat(np.asarray(inp["rwkv_r_k"][l]).reshape(-1))
        for j in range(4):
            o, n = PVL["cw%d" % j]
            pvec[:, l, o:o + n] = _feat(inp["lru_conv_w"][l][j])
    lr3 = np.zeros((128, L, 3, 256), np.float32)
    lruw = np.zeros((128, L, 2, 2, 128), np.float32)
    for l in range(L):
        lr3[0:32, l, 0, :] = inp["rwkv_w2"][l]
        lr3[32:64, l, 1, :] = inp["rwkv_a2"][l]
        lr3[64:128, l, 2, :] = inp["rwkv_g2"][l]
        for i, nm in enumerate(["lru_wa", "lru_wx"]):
            w = np.asarray(inp[nm][l])
            for n in range(4):
                ch, hb = n // 2, n % 2
                lruw[hb * 64:(hb + 1) * 64, l, i, ch, hb * 64:(hb + 1) * 64] = w[n]
    ki = np.arange(128)[:, None]
    qi = np.arange(128)[None, :]
    biasT = np.zeros((128, L, 5, 4, 128), np.float32)
    for j in range(5):
        rel = (-8 + 2 * j) * 64 + ki - qi
        kc = -8 + 2 * j + ki // 64
        cq = qi // 64
        ok = ((cq - kc) >= 0) & ((cq - kc) <= 8)
        idx = np.clip(rel, -256, 256) + 256
        for l in range(L):
            tab = np.asarray(inp["attn_rel_bias"][l], np.float32)
            for h in range(4):
                biasT[:, l, j, h, :] = np.where(ok, tab[h][idx], np.float32(-30000.0))
    return pvec, lr3, lruw, biasT


def run(inputs, S_LEN, L, n_cores=8, **kw):
    x = np.asarray(inputs["x"], np.float32)
    B = x.shape[0]
    pvec, lr3, lruw, biasT = pack_params(inputs, L)
    consts = make_consts()
    nc = build(S_LEN, L, **kw)
    shared = dict(w_in=np.ascontiguousarray(inputs["w_in"][:L], dtype=np.float32),
                  w_out=np.ascontiguousarray(inputs["w_out"][:L], dtype=np.float32),
                  mlp_w1=np.ascontiguousarray(inputs["mlp_w1"][:L], dtype=np.float32),
                  mlp_w2=np.ascontiguousarray(inputs["mlp_w2"][:L], dtype=np.float32),
                  pvec=pvec, lr3=lr3, lruw=lruw, biasT=biasT, consts=consts)
    in_maps = []
    for c in range(n_cores):
        b = c % B
        m = dict(shared)
        m["xT"] = np.ascontiguousarray(x[b, :S_LEN].T)
        in_maps.append(m)
    res = run_bass_kernel_spmd(nc, in_maps, core_ids=list(range(n_cores)))
    return res


def kernel(**inputs):
    x = np.asarray(inputs["x"])
    B, S_LEN, _ = x.shape
    L = np.asarray(inputs["w_in"]).shape[0]
    res = run(inputs, S_LEN, L)
    out = np.stack([np.ascontiguousarray(res.results[b]["yT"].T) for b in range(B)], 0)
    return out.astype(np.float32)
```
